# Optimizing a Trainium2 kernel written in Bass

```python
import jax, jax.numpy as jnp
from jax import lax
import numpy as np

D_MODEL = 1024
BATCH = 2
SEQ = 8192
DEPTH = 4

N_A = DEPTH // 2
N_B = DEPTH - N_A
POOL_WINDOWS = (2, 4, 8, 16)
POOL_GROUPS = len(POOL_WINDOWS)
GC = D_MODEL // POOL_GROUPS
HEAD_DIM = 64
N_HEADS = D_MODEL // HEAD_DIM
N_KV = max(1, N_HEADS // 8)
GROUP = N_HEADS // N_KV
WINDOW = 128
BLK = WINDOW
D_FF = 2816
CONV_W = 3
EPS = 1e-5

kernel_name = "yoco_pool_swa_sink_convffn"


def rmsnorm(x, g):
    xf = x.astype(jnp.float32)
    y = xf * lax.rsqrt(jnp.mean(xf * xf, axis=-1, keepdims=True) + EPS)
    return (y * g.astype(jnp.float32)).astype(x.dtype)


def pool_mixer(h, w, scale):
    B, S, D = h.shape
    c = jnp.cumsum(h.astype(jnp.float32), axis=1)
    t = jnp.arange(S)
    pooled = []
    for gi, win in enumerate(POOL_WINDOWS):
        cg = c[..., gi * GC:(gi + 1) * GC]
        lag = jnp.pad(cg, ((0, 0), (win, 0), (0, 0)))[:, :S]
        cnt = jnp.minimum(t + 1, win).astype(jnp.float32)[None, :, None]
        pooled.append((cg - lag) / cnt)
    pooled = jnp.stack(pooled, axis=2).astype(h.dtype) - h.reshape(B, S, POOL_GROUPS, GC)
    y = jnp.einsum('bsgc,gcd->bsgd', pooled, w).reshape(B, S, D)
    return y * scale


def conv_ffn(h, w_up, conv_w, conv_b, w_down):
    u = h @ w_up
    S = u.shape[1]
    up = jnp.pad(u, ((0, 0), (CONV_W - 1, 0), (0, 0)))
    u = sum(conv_w[k] * up[:, k:k + S] for k in range(CONV_W)) + conv_b
    gate, val = jnp.split(u, 2, axis=-1)
    return (jax.nn.silu(gate) * val) @ w_down


def swa_sink_attention(q, k, v, sinks):
    B, S = q.shape[:2]
    nb = S // BLK
    qb = q.reshape(B, nb, BLK, N_KV, GROUP, HEAD_DIM)
    kb = k.reshape(B, nb, BLK, N_KV, HEAD_DIM)
    vb = v.reshape(B, nb, BLK, N_KV, HEAD_DIM)
    pad = ((0, 0), (1, 0), (0, 0), (0, 0), (0, 0))
    kw = jnp.concatenate([jnp.pad(kb, pad)[:, :nb], kb], axis=2)
    vw = jnp.concatenate([jnp.pad(vb, pad)[:, :nb], vb], axis=2)
    s = jnp.einsum('bnqkgd,bnskd->bnkgqs', qb, kw).astype(jnp.float32) * (HEAD_DIM ** -0.5)
    qi = jnp.arange(BLK)[:, None]
    si = jnp.arange(2 * BLK)[None, :]
    band = (si > qi) & (si <= qi + BLK)
    valid = (jnp.arange(nb)[:, None, None] > 0) | (si >= BLK)[None]
    mask = (band[None] & valid)[None, :, None, None]
    sink = sinks.astype(jnp.float32).reshape(N_KV, GROUP)[None, None, :, :, None, None]
    s = jnp.where(mask, s, -jnp.inf)
    m = jnp.maximum(jnp.max(s, axis=-1, keepdims=True), sink)
    p = jnp.exp(s - m)
    denom = jnp.sum(p, axis=-1, keepdims=True) + jnp.exp(sink - m)
    pr = (p / denom).astype(v.dtype)
    o = jnp.einsum('bnkgqs,bnskd->bnqkgd', pr, vw)
    return o.reshape(B, S, N_HEADS * HEAD_DIM)


def setup_inputs(seed: int = 0) -> dict:
    key = jax.random.key(seed)
    ks = jax.random.split(key, 20)
    f32 = jnp.float32
    nrm = lambda k, shp, s: jax.random.normal(k, shp, f32) * s
    KVW = 2 * N_KV * HEAD_DIM
    QW = N_HEADS * HEAD_DIM
    return {
        "x": nrm(ks[0], (BATCH, SEQ, D_MODEL), 1.0),
        "norm1_g": 1.0 + nrm(ks[1], (DEPTH, D_MODEL), 0.02),
        "norm2_g": 1.0 + nrm(ks[2], (DEPTH, D_MODEL), 0.02),
        "pool_w": nrm(ks[3], (N_A, POOL_GROUPS, GC, GC), GC ** -0.5),
        "pool_scale": 1.0 + nrm(ks[4], (N_A, D_MODEL), 0.02),
        "kv_norm_g": 1.0 + nrm(ks[5], (D_MODEL,), 0.02),
        "w_kv": nrm(ks[6], (D_MODEL, KVW), D_MODEL ** -0.5),
        "b_kv": nrm(ks[7], (KVW,), 0.02),
        "w_q": nrm(ks[8], (N_B, D_MODEL, QW), D_MODEL ** -0.5),
        "b_q": nrm(ks[9], (N_B, QW), 0.02),
        "sinks": nrm(ks[10], (N_B, N_HEADS), 1.0),
        "w_o": nrm(ks[11], (N_B, QW, D_MODEL), QW ** -0.5),
        "b_o": nrm(ks[12], (N_B, D_MODEL), 0.02),
        "ffn_up": nrm(ks[13], (DEPTH, D_MODEL, 2 * D_FF), D_MODEL ** -0.5),
        "ffn_conv_w": nrm(ks[14], (DEPTH, CONV_W, 2 * D_FF), CONV_W ** -0.5),
        "ffn_conv_b": nrm(ks[15], (DEPTH, 2 * D_FF), 0.02),
        "ffn_down": nrm(ks[16], (DEPTH, D_FF, D_MODEL), D_FF ** -0.5),
        "final_g": 1.0 + nrm(ks[17], (D_MODEL,), 0.02),
    }


def reference(x, norm1_g, norm2_g, pool_w, pool_scale, kv_norm_g, w_kv, b_kv, w_q, b_q,
              sinks, w_o, b_o, ffn_up, ffn_conv_w, ffn_conv_b, ffn_down, final_g):
    B, S, D = x.shape
    k_sh = v_sh = None
    for l in range(DEPTH):
        h = rmsnorm(x, norm1_g[l])
        if l < N_A:
            x = x + pool_mixer(h, pool_w[l], pool_scale[l])
        else:
            j = l - N_A
            q = (h @ w_q[j] + b_q[j]).reshape(B, S, N_HEADS, HEAD_DIM)
            o = swa_sink_attention(q, k_sh, v_sh, sinks[j])
            x = x + (o @ w_o[j] + b_o[j])
        h = rmsnorm(x, norm2_g[l])
        x = x + conv_ffn(h, ffn_up[l], ffn_conv_w[l], ffn_conv_b[l], ffn_down[l])
        if l == N_A - 1:
            kv = rmsnorm(x, kv_norm_g) @ w_kv + b_kv
            k_sh, v_sh = jnp.split(kv.reshape(B, S, 2 * N_KV, HEAD_DIM), 2, axis=2)
    return rmsnorm(x, final_g)
```

```python
import numpy as np
from contextlib import ExitStack
import concourse.bass as bass
import concourse.mybir as mybir
from concourse.bass_utils import run_bass_kernel_spmd

F32 = mybir.dt.float32
BF16 = mybir.dt.bfloat16
AF = mybir.ActivationFunctionType
ALU = mybir.AluOpType

NCORES = 8
D = 1024
NCH = 8
SEQ = 8192
TOWN = 2048
HALO = 256
T = TOWN + HALO
NBLK = T // 128
DFF = 2816
NFC = 22
DEPTH = 4
N_A = 2
EPS = 1e-5
HOFF = 16
MACRO = 768
SUB = 384
NSLOT = 3
PAD = 3
SLOT_ELEMS = 4096
MAXPARTS = 5
HALF_PAIRS = (12, 10)

ENG_NAMES = ["pe", "act", "dve", "pool", "sp"]


class Op:
    __slots__ = ("eng", "fn", "is_dma", "semkey", "waits", "signal", "count", "idx", "group")


class Prog:
    def __init__(self):
        self.ops = []
        self.last_writer = {}
        self.readers = {}

    def add(self, eng, fn, reads=(), writes=(), dma=None, group=None):
        op = Op()
        op.group = group
        op.eng = eng
        op.fn = fn
        op.is_dma = dma is not None
        op.semkey = dma
        op.signal = False
        op.count = None
        op.idx = len(self.ops)
        deps = set()
        for k in reads:
            w = self.last_writer.get(k)
            if w is not None:
                deps.add(w)
        for k in writes:
            w = self.last_writer.get(k)
            if w is not None:
                deps.add(w)
            for r in self.readers.get(k, ()):
                deps.add(r)
        if eng == "pe" and dma is None:
            deps = {d for d in deps if not (self.ops[d].eng == "pe" and not self.ops[d].is_dma)}
        op.waits = deps
        for k in reads:
            self.readers.setdefault(k, []).append(op.idx)
        for k in writes:
            self.last_writer[k] = op.idx
            self.readers[k] = []
        self.ops.append(op)
        return op

    def dma_keys(self):
        return sorted({op.semkey for op in self.ops if op.is_dma})

    def emit(self, sems, dma_sems, block):
        ops = self.ops
        for op in ops:
            for d in op.waits:
                ops[d].signal = True
        cnt = {}
        for op in ops:
            key = ("dma", op.semkey) if op.is_dma else ("eng", op.eng)
            if op.is_dma or op.signal:
                cnt[key] = cnt.get(key, 0) + (16 if op.is_dma else 1)
                op.count = (key, cnt[key])
        gmax = {}
        for op in ops:
            if op.is_dma and op.group is not None:
                gk = (op.semkey, op.group)
                gmax[gk] = max(gmax.get(gk, 0), op.count[1])
        for op in ops:
            if op.is_dma and op.group is not None:
                op.count = (op.count[0], gmax[(op.semkey, op.group)])
        per_eng = {e: [] for e in ENG_NAMES}
        for op in ops:
            per_eng[op.eng].append(op)

        def sem_of(key):
            return dma_sems[key[1]] if key[0] == "dma" else sems[key[1]]

        def run(engname, e):
            seen = {}
            for op in per_eng[engname]:
                need = {}
                for d in op.waits:
                    key, v = ops[d].count
                    if need.get(key, 0) < v:
                        need[key] = v
                for key, v in need.items():
                    if seen.get(key, 0) >= v:
                        continue
                    seen[key] = v
                    e.wait_ge(sem_of(key), v)
                ins = op.fn(e)
                if op.count is not None:
                    ins.then_inc(sem_of(op.count[0]), 16 if op.is_dma else 1)

        @block.tensor
        def _(e):
            run("pe", e)

        @block.scalar
        def _(e):
            run("act", e)

        @block.vector
        def _(e):
            run("dve", e)

        @block.gpsimd
        def _(e):
            run("pool", e)

        @block.sync
        def _(e):
            run("sp", e)


def vec_layout():
    lay = {}
    off = 0
    for name, n in [("n1g", 32), ("n2g", 32), ("psc", 16), ("kvg", 8), ("fing", 8), ("bq", 16),
                    ("bo", 16), ("cw", 4 * 3 * 44), ("cb", 4 * 44), ("bk", 4), ("flag", 1)]:
        lay[name] = off
        off += n
    return lay, off


def rep_layout():
    lay = {}
    off = 0
    for name, n in [("bv", 128), ("sink", 32), ("invcnt", 128)]:
        lay[name] = off
        off += n
    return lay, off


def _cols(a):
    a = np.asarray(a, dtype=np.float32)
    lead = int(np.prod(a.shape[:-1])) if a.ndim > 1 else 1
    n = a.shape[-1] // 128
    return np.ascontiguousarray(a.reshape(lead, n, 128).transpose(2, 0, 1).reshape(128, lead * n))


def host_prep(inputs):
    x = np.asarray(inputs["x"], dtype=np.float32)
    vl, nv = vec_layout()
    rl, nr = rep_layout()
    common_vec = np.zeros((128, nv), np.float32)

    def put(name, arr):
        common_vec[:, vl[name]:vl[name] + arr.shape[1]] = arr

    put("n1g", _cols(inputs["norm1_g"]))
    put("n2g", _cols(inputs["norm2_g"]))
    put("psc", _cols(inputs["pool_scale"]))
    put("kvg", _cols(inputs["kv_norm_g"]))
    put("fing", _cols(inputs["final_g"]))
    put("bq", _cols(inputs["b_q"]))
    put("bo", _cols(inputs["b_o"]))
    put("cw", _cols(np.asarray(inputs["ffn_conv_w"]).reshape(12, 5632)))
    put("cb", _cols(inputs["ffn_conv_b"]))
    bkv = np.asarray(inputs["b_kv"], np.float32)
    z64 = np.zeros(64, np.float32)
    bk = np.stack([np.concatenate([bkv[0:64], z64]), np.concatenate([z64, bkv[0:64]]),
                   np.concatenate([bkv[64:128], z64]), np.concatenate([z64, bkv[64:128]])], axis=1)
    put("bk", bk)

    s_idx = np.arange(128)[:, None]
    q_idx = np.arange(128)[None, :]
    m_prev = (s_idx > q_idx).astype(np.float32)
    m_diag = (s_idx <= q_idx).astype(np.float32)
    ident = np.eye(128, dtype=np.float32)

    weights = {k: np.ascontiguousarray(np.asarray(inputs[k], dtype=np.float32))
               for k in ["pool_w", "w_kv", "w_q", "w_o", "ffn_up", "ffn_down"]}
    wkv = weights["w_kv"]
    wk_pad = np.zeros((D, 2, 2, 128), np.float32)
    for g in range(2):
        wk_pad[:, g, 0, 0:64] = wkv[:, g * 64:(g + 1) * 64]
        wk_pad[:, g, 1, 64:128] = wkv[:, g * 64:(g + 1) * 64]
    weights["wk_pad"] = wk_pad.reshape(D, 512)
    in_maps = []
    for core in range(NCORES):
        b = core // 4
        t0 = (core % 4) * TOWN
        first = (core % 4) == 0
        xt = np.zeros((D, T), np.float32)
        if first:
            xt[:, HALO:] = x[b, 0:TOWN, :].T
        else:
            xt[:, :] = x[b, t0 - HALO:t0 + TOWN, :].T
        vec = common_vec.copy()
        vec[:, vl["flag"]] = 0.0 if first else 1.0
        rep = np.zeros((128, nr), np.float32)
        rep[:, rl["bv"]:rl["bv"] + 128] = bkv[128:256][None, :]
        rep[:, rl["sink"]:rl["sink"] + 32] = np.asarray(inputs["sinks"], np.float32).reshape(1, 32)
        inv = np.zeros((8, 16), np.float32)
        for c in range(8):
            win = 2 ** (c // 2 + 1)
            for tt in range(16):
                cnt = min(tt + 1, win) if first else win
                inv[c, tt] = 1.0 / cnt
        rep[:, rl["invcnt"]:rl["invcnt"] + 128] = inv.reshape(1, 128)
        masks = np.stack([m_prev, m_diag, m_prev * (0.0 if first else 1.0), m_diag], axis=1)
        m = {"xT": xt, "vecs": vec, "reps": rep, "masks": np.ascontiguousarray(masks), "ident": ident}
        m.update(weights)
        in_maps.append(m)
    return in_maps


POOL_START = 64
ATTN_FFN_START = 248


def macro_tiles(kind):
    res = []
    for m in range(T // MACRO):
        t0 = m * MACRO
        subs = [(t0, SUB), (t0 + SUB, SUB)]
        if m == 0:
            first = {"pool": POOL_START, "attn": 128, "attn_ffn": ATTN_FFN_START, "std": 0}[kind]
            subs = [(first, SUB - first), (SUB, SUB)]
        res.append((t0, subs))
    return res


def build_program(stop_after=None):
    nc = bass.Bass("TRN2", target_bir_lowering=False)
    vl, nv = vec_layout()
    rl, nr = rep_layout()
    dr = {}
    dr["xT"] = nc.dram_tensor("xT", [D, T], F32, kind="ExternalInput").ap()
    dr["vecs"] = nc.dram_tensor("vecs", [128, nv], F32, kind="ExternalInput").ap()
    dr["reps"] = nc.dram_tensor("reps", [128, nr], F32, kind="ExternalInput").ap()
    dr["masks"] = nc.dram_tensor("masks", [128, 4, 128], F32, kind="ExternalInput").ap()
    dr["ident"] = nc.dram_tensor("ident", [128, 128], F32, kind="ExternalInput").ap()
    dr["pool_w"] = nc.dram_tensor("pool_w", [2, 4, 256, 256], F32, kind="ExternalInput").ap()
    dr["w_kv"] = nc.dram_tensor("w_kv", [1024, 256], F32, kind="ExternalInput").ap()
    dr["wk_pad"] = nc.dram_tensor("wk_pad", [1024, 512], F32, kind="ExternalInput").ap()
    dr["w_q"] = nc.dram_tensor("w_q", [2, 1024, 1024], F32, kind="ExternalInput").ap()
    dr["w_o"] = nc.dram_tensor("w_o", [2, 1024, 1024], F32, kind="ExternalInput").ap()
    dr["ffn_up"] = nc.dram_tensor("ffn_up", [4, 1024, 5632], F32, kind="ExternalInput").ap()
    dr["ffn_down"] = nc.dram_tensor("ffn_down", [4, 2816, 1024], F32, kind="ExternalInput").ap()
    if stop_after is None:
        yT = nc.dram_tensor("yT", [D, TOWN], F32, kind="ExternalOutput").ap()
    else:
        dbg = nc.dram_tensor("dbg", [D, T], F32, kind="ExternalOutput").ap()

    P = Prog()
    with ExitStack() as st:
        def sb(name, shape, dt):
            return st.enter_context(nc.sbuf_tensor(name, shape, dt))

        def psb(name, shape, dt):
            return st.enter_context(nc.psum_tensor(name, shape, dt))

        x_sb = sb("x_sb", [128, NCH, T], F32)
        hn = sb("hn", [128, NCH, HOFF + MACRO], BF16)
        G = sb("G", [128, 12, MACRO], BF16)
        pooled = sb("pooled", [128, NCH, MACRO], BF16)
        hn1 = sb("hn1", [128, NCH, HOFF + MACRO], BF16)
        _pf = pooled[:].rearrange("p c t -> p (c t)")
        Pt = [_pf[:, i * 512:(i + 1) * 512] for i in range(4)]
        Osb = [_pf[:, 2048 + i * 1024:2048 + (i + 1) * 1024].rearrange("p (h d) -> p h d", h=16) for i in range(2)]
        wr = [sb(f"wr{i}", [128, SLOT_ELEMS], BF16) for i in range(NSLOT)]
        KT = sb("KT", [128, 2, 2, T], BF16)
        Vaug = sb("Vaug", [128, NBLK, 2, 65], BF16)
        ftmp = [sb(f"ftmp{i}", [128, SUB], F32) for i in range(6)]
        sq = sb("sq", [128, NCH, SUB], BF16)
        rstd2 = sb("rstd", [128, 2, SUB], F32)
        rtmp = sb("rtmp", [128, SUB], F32)
        epsc = sb("epsc", [128, 1], F32)
        pA = sb("pA", [128, HOFF + MACRO], F32)
        pB = sb("pB", [128, HOFF + MACRO], F32)
        ptmp = sb("ptmp", [128, 16], F32)
        carry1 = sb("carry1", [128, NCH, 16], BF16)
        carry2 = sb("carry2", [128, NCH, 2], BF16)
        NPT = 4
        den = sb("den", [128, 16], F32)
        rden = sb("rden", [128, 16], F32)
        esink = sb("esink", [128, 32], F32)
        vecs = sb("vecs_sb", [128, nv], F32)
        reps = sb("reps_sb", [128, nr], F32)
        masks = sb("masks_sb", [128, 4, 128], BF16)
        ident = sb("ident_sb", [128, 128], BF16)
        ones = sb("ones_sb", [128, 128], BF16)

        pa = [psb(f"pa{i}", [128, 512], F32) for i in range(4)]
        pb = [psb(f"pb{i}", [128, 512], F32) for i in range(2)]
        pn = psb("pn", [128, 512], F32)
        pt = psb("pt", [128, 1024], BF16)

        def vcol(name, i):
            o = vl[name] + i
            return vecs[:, o:o + 1]

        def mm(out, lhsT, rhs, start, stop, reads, writes):
            P.add("pe", lambda e: e.matmul(out, lhsT=lhsT, rhs=rhs, start=start, stop=stop), reads, writes)

        def tr(out, in_, reads, writes):
            P.add("pe", lambda e: e.transpose(out, in_, ident[:]), list(reads) + ["ident"], writes)

        def act(out, in_, func, reads, writes, bias=None, scale=None):
            kw = {}
            if bias is not None:
                kw["bias"] = bias
            if scale is not None:
                kw["scale"] = scale
            P.add("act", lambda e: e.activation(out=out, in_=in_, func=func, **kw), reads, writes)

        def ts(eng, out, in0, s1, s2, op0, op1, reads, writes):
            if op1 is None:
                P.add(eng, lambda e: e.tensor_scalar(out, in0, s1, None, op0), reads, writes)
            else:
                P.add(eng, lambda e: e.tensor_scalar(out, in0, s1, s2, op0, op1), reads, writes)

        def stt(eng, out, in0, scalar, in1, op0, op1, reads, writes):
            P.add(eng, lambda e: e.scalar_tensor_tensor(out, in0, scalar, in1, op0, op1), reads, writes)

        def tt(eng, out, in0, in1, op, reads, writes):
            P.add(eng, lambda e: e.tensor_tensor(out, in0, in1, op), reads, writes)

        def cp(eng, out, in_, reads, writes):
            P.add(eng, lambda e: e.tensor_copy(out, in_), reads, writes)

        def mset(eng, ap, val, writes):
            P.add(eng, lambda e: e.memset(ap, val), (), writes)

        def dma(q, out, in_, semkey, reads, writes, group=None):
            P.add(q, lambda e: e.dma_start(out=out, in_=in_), reads, writes, dma=semkey, group=group)

        def xkeys(cs, t0, ln):
            return [("x", c, b) for c in cs for b in range(t0 // 128, (t0 + ln + 127) // 128)]

        ALLC = list(range(NCH))

        ring = {"plan": [], "issued": 0, "cur": 0}

        def slot_keys(s):
            return [("ws", s, i) for i in range(MAXPARTS)]

        def issue_upto(k):
            while ring["issued"] < min(k, len(ring["plan"])):
                i = ring["issued"]
                s = i % NSLOT
                tag, parts = ring["plan"][i]
                for pi, (ovf, iv) in enumerate(parts):
                    wk = [("ws", s, pi)]
                    if pi == len(parts) - 1:
                        wk = [("ws", s, j) for j in range(pi, MAXPARTS)]
                    dma("pool", ovf(wr[s]), iv, f"ws{s}", (), wk, group=i)
                ring["issued"] += 1

        def acquire(tag):
            k = ring["cur"]
            assert ring["plan"][k][0] == tag, (ring["plan"][k][0], tag)
            issue_upto(k + NSLOT)
            ring["cur"] += 1
            s = k % NSLOT
            return wr[s], slot_keys(s)

        def spec_pool_w(l):
            return (("poolw", l), [(lambda w: w[:, 0:2048].rearrange("p (g k d) -> p g k d", g=4, k=2),
                                    dr["pool_w"][l].rearrange("g (k p) d -> p g k d", p=128))])

        def spec_up(l, pp):
            upv = dr["ffn_up"][l].rearrange("(k p) f -> p k f", p=128)
            parts = []
            for gv in range(2):
                c0 = gv * DFF + pp * 256
                parts.append(((lambda gv: lambda w: w[:, 0:4096].rearrange("p (k g c) -> p k g c", k=8, g=2)[:, :, gv, :])(gv),
                              upv[:, :, c0:c0 + 256]))
            return (("up", l, pp), parts)

        def spec_down(l, half, dq):
            f0 = 0 if half == 0 else HALF_PAIRS[0]
            nf = HALF_PAIRS[half]
            dv = dr["ffn_down"][l].rearrange("(f p) d -> p f d", p=128)
            return (("down", l, half, dq), [((lambda nf: lambda w: w[:, 0:nf * 256].rearrange("p (f d) -> p f d", f=nf))(nf),
                                             dv[:, f0:f0 + nf, dq * 256:(dq + 1) * 256])])

        def spec_wq(j, hf):
            v = dr["w_q"][j].rearrange("(k p) f -> p k f", p=128)
            return (("wq", j, hf), [(lambda w: w[:, 0:4096].rearrange("p (k c) -> p k c", k=8), v[:, :, hf * 512:(hf + 1) * 512])])

        def spec_wo(j, hf):
            v = dr["w_o"][j].rearrange("(k p) f -> p k f", p=128)
            return (("wo", j, hf), [(lambda w: w[:, 0:4096].rearrange("p (k c) -> p k c", k=8), v[:, :, hf * 512:(hf + 1) * 512])])

        def spec_wkv():
            v = dr["w_kv"].rearrange("(k p) f -> p k f", p=128)
            vk = dr["wk_pad"].rearrange("(k p) f -> p k f", p=128)
            return [(("wkvK",), [(lambda w: w[:, 0:4096].rearrange("p (k c) -> p k c", k=8), vk)]),
                    (("wkvV",), [(lambda w: w[:, 0:1024].rearrange("p (k c) -> p k c", k=8), v[:, :, 128:256])])]

        def stop_here(l, stage):
            return stop_after is not None and tuple(stop_after) == (l, stage)

        done = False
        for l in range(DEPTH):
            attn = l >= N_A
            for (t0, subs) in macro_tiles("attn" if attn else "pool"):
                if attn:
                    for _ in subs:
                        ring["plan"] += [spec_wq(l - N_A, 0), spec_wq(l - N_A, 1), spec_wo(l - N_A, 0), spec_wo(l - N_A, 1)]
                else:
                    ring["plan"].append(spec_pool_w(l))
                if not (stop_after is not None and tuple(stop_after) == (l, "M")):
                    pj = 0
                    for half in range(2):
                        for _ in range(HALF_PAIRS[half] // 2):
                            ring["plan"].append(spec_up(l, pj))
                            pj += 1
                        for dq in range(4):
                            ring["plan"].append(spec_down(l, half, dq))
                    if l == N_A - 1:
                        ring["plan"] += spec_wkv()
            if stop_after is not None and stop_after[0] == l:
                break

        xv = dr["xT"].rearrange("(c p) t -> p c t", p=128)
        for m in range(T // MACRO):
            for c in range(NCH):
                dma("sp", x_sb[:, c, m * MACRO:(m + 1) * MACRO], xv[:, c, m * MACRO:(m + 1) * MACRO], f"xl{m}",
                    (), xkeys([c], m * MACRO, MACRO), group=m)
            if m == 0:
                dma("sp", vecs[:], dr["vecs"], "cvec", (), ["vecs"])
                dma("sp", reps[:], dr["reps"], "crep", (), ["reps"])
        dma("pool", masks[:], dr["masks"], "cmask", (), ["masks"])
        dma("pool", ident[:], dr["ident"], "cident", (), ["ident"])
        issue_upto(NSLOT)
        mset("dve", ones[:], 1.0 / 1024.0, ["ones"])
        mset("pool", hn[:], 0.0, [("hn", c, s_) for c in ALLC for s_ in range(2)] + [("hnh", c) for c in ALLC])
        mset("pool", hn1[:], 0.0, [("h1", c, s_) for c in ALLC for s_ in range(2)] + [("h1h", c) for c in ALLC])
        mset("dve", epsc[:], EPS, ["epsc"])
        mset("dve", Vaug[:, :, :, 64:65], 1.0, ["vones"])
        act(esink[:], reps[:, rl["sink"]:rl["sink"] + 32], AF.Exp, ["reps"], ["esink"])

        def rsqrt_pn(sl, ri=0):
            rstd = rstd2[:, ri, :]
            rk = ("rstd", ri)
            act(rtmp[:, 0:sl], pn[:, 0:sl], AF.Ln, ["pn", "epsc"], ["rtmp"], bias=epsc[:, 0:1])
            act(rstd[:, 0:sl], rtmp[:, 0:sl], AF.Exp, ["rtmp"], [rk], scale=-0.5)

        def norm_stt(c, s0, sl, col0, si, dst, dkey, gname, gidx0):
            return lambda: stt("dve", dst[:, c, col0:col0 + sl], x_sb[:, c, s0:s0 + sl], vcol(gname, gidx0 + c), rstd2[:, si, 0:sl],
                               ALU.mult, ALU.mult, xkeys([c], s0, sl) + [("rstd", si), "vecs"], [(dkey, c, si)])

        def norm(s0, sl, t0, gname, gidx0, defer=False, dst=None, dkey="hn", pad=0, stt_later=False):
            dst = hn if dst is None else dst
            si = 0 if s0 < t0 + SUB else 1
            col0 = HOFF + (s0 - t0)
            todo = []
            D_ = todo.append
            D_(lambda: act(sq[:, 0:4, 0:sl], x_sb[:, 0:4, s0:s0 + sl], AF.Square, xkeys(range(0, 4), s0, sl), [("sq", 0)]))
            D_(lambda: act(sq[:, 4:8, 0:sl], x_sb[:, 4:8, s0:s0 + sl], AF.Square, xkeys(range(4, 8), s0, sl), [("sq", 1)]))
            for _ in range(pad):
                D_(lambda: None)

            def mms():
                for c in range(NCH):
                    mm(pn[:, 0:sl], ones[:], sq[:, c, 0:sl], c == 0, c == NCH - 1, [("sq", c // 4), "ones"], ["pn"])
            D_(mms)
            D_(lambda: rsqrt_pn(sl, si))
            if stt_later:
                return todo
            for c in range(NCH):
                D_(norm_stt(c, s0, sl, col0, si, dst, dkey, gname, gidx0))
            if defer:
                return todo
            for f in todo:
                f()
            return si

        def zero_halo(t0, subs):
            if t0 != 0:
                return
            lo = subs[0][0]
            ts("dve", x_sb[:, :, lo:HALO], x_sb[:, :, lo:HALO], vcol("flag", 0), None, ALU.mult, None,
               xkeys(ALLC, lo, HALO - lo) + ["vecs"], xkeys(ALLC, lo, HALO - lo))

        def pool_pre(l, t0, subs):
            L = MACRO
            W = HOFF + L
            todo = []
            D_ = todo.append
            if t0 == 0:
                D_(lambda: mset("pool", hn1[:, :, 0:HOFF], 0.0, [("h1h", c) for c in ALLC]))
            else:
                D_(lambda: cp("pool", hn1[:, :, 0:HOFF], carry1[:], ["carry1"], [("h1h", c) for c in ALLC]))
            for (s0, sl) in subs:
                todo.extend(norm(s0, sl, t0, "n1g", l * 8, defer=True, dst=hn1, dkey="h1", pad=PAD, stt_later=True))
            for c in range(NCH):
                nl = c // 2 + 1
                win = 2 ** nl
                hk = [("h1", c, 0), ("h1", c, 1), ("h1h", c)]
                for (s0, sl) in subs:
                    si_ = 0 if s0 < t0 + SUB else 1
                    D_(norm_stt(c, s0, sl, HOFF + (s0 - t0), si_, hn1, "h1", "n1g", l * 8))
                src = hn1[:, c, :]
                bufs = [pA, pB]
                cur = None
                for lev in range(nl):
                    sh = 2 ** lev
                    lo = 2 * sh - 1
                    dst = bufs[lev % 2]
                    dk = "pA" if lev % 2 == 0 else "pB"
                    if lev == 0:
                        D_((lambda dst, lo, sh, hk, dk, src: lambda: tt("dve", dst[:, lo:W], src[:, lo:W], src[:, lo - sh:W - sh], ALU.add, hk, [dk]))(dst, lo, sh, hk, dk, src))
                    else:
                        ck = "pA" if (lev - 1) % 2 == 0 else "pB"
                        D_((lambda dst, lo, sh, ck, dk, cur: lambda: tt("dve", dst[:, lo:W], cur[:, lo:W], cur[:, lo - sh:W - sh], ALU.add, [ck], [dk]))(dst, lo, sh, ck, dk, cur))
                    cur = dst
                ck = "pA" if (nl - 1) % 2 == 0 else "pB"
                D_((lambda c, cur, win, src, ck, hk: lambda: stt("dve", pooled[:, c, 0:L], cur[:, HOFF:W], 1.0 / win, src[:, HOFF:W], ALU.mult, ALU.subtract,
                                                                 [ck] + hk, [("pl", c, 0), ("pl", c, 1)]))(c, cur, win, src, ck, hk))
                if t0 == 0:
                    io = rl["invcnt"] + c * 16
                    D_((lambda c, cur, io, ck: lambda: tt("dve", ptmp[:], cur[:, HOFF + HALO:HOFF + HALO + 16], reps[:, io:io + 16], ALU.mult,
                                                          [ck, "reps"], ["ptmp"]))(c, cur, io, ck))
                    D_((lambda c, src, hk: lambda: tt("dve", pooled[:, c, HALO:HALO + 16], ptmp[:], src[:, HOFF + HALO:HOFF + HALO + 16], ALU.subtract,
                                                      ["ptmp"] + hk, [("pl", c, 0)]))(c, src, hk))
            D_(lambda: cp("pool", carry1[:], hn1[:, :, HOFF + L - 16:HOFF + L], [("h1", c, 1) for c in ALLC], ["carry1"]))
            return todo

        def pool_post(l, t0, subs):
            w, wk = acquire(("poolw", l))
            pw = w[:, 0:2048].rearrange("p (g k d) -> p g k d", g=4, k=2)
            for (s0, sl) in subs:
                si = 0 if s0 < t0 + SUB else 1
                c0 = s0 - t0
                for dc in range(NCH):
                    g = dc // 2
                    ps = pb[dc % 2]
                    for kc in range(2):
                        mm(ps[:, 0:sl], pw[:, g, kc, (dc % 2) * 128:(dc % 2) * 128 + 128], pooled[:, 2 * g + kc, c0:c0 + sl],
                           kc == 0, kc == 1, wk + [("pl", 2 * g + kc, si)], [("pb", dc % 2)])
                    stt("dve", x_sb[:, dc, s0:s0 + sl], ps[:, 0:sl], vcol("psc", l * 8 + dc), x_sb[:, dc, s0:s0 + sl],
                        ALU.mult, ALU.add, [("pb", dc % 2), "vecs"] + xkeys([dc], s0, sl), xkeys([dc], s0, sl))
            zero_halo(t0, subs)

        def ffn_stage(l, t0, subs, first_tile, inject=None):
            L = MACRO
            if first_tile:
                mset("pool", hn[:, :, HOFF - 2:HOFF], 0.0, [("hnh", c) for c in ALLC])
            else:
                cp("pool", hn[:, :, HOFF - 2:HOFF], carry2[:], ["carry2"], [("hnh", c) for c in ALLC])
            for (s0, sl) in subs:
                norm(s0, sl, t0, "n2g", l * 8)
            cp("pool", carry2[:], hn[:, :, HOFF + L - 2:HOFF + L], [("hn", c, 1) for c in ALLC], ["carry2"])
            hsub = [[[("hn", c, 0), ("hnh", c)] for c in ALLC],
                    [[("hn", c, 1), ("hn", c, 0)] for c in ALLC]]
            cwb = l * 3 * 44
            pj_base = 0
            fbuf = 0
            for half in range(2):
                npair = HALF_PAIRS[half]
                for pp in range(npair // 2):
                    w, wk = acquire(("up", l, pj_base // 2 + pp))
                    upw = w[:, 0:4096].rearrange("p (k g c) -> p k g c", k=8, g=2)
                    for (s0, sl) in subs:
                        for pj in range(2):
                            j = pj_base + pp * 2 + pj
                            jj = pp * 2 + pj
                            si = 0 if s0 < t0 + SUB else 1
                            c0 = HOFF + (s0 - t0)
                            pg = pa[2 * fbuf]
                            pv = pa[2 * fbuf + 1]
                            Ag, Av, Sg = ftmp[3 * fbuf], ftmp[3 * fbuf + 1], ftmp[3 * fbuf + 2]
                            kg, kv_, kA, kV, kS = ("pa", 2 * fbuf), ("pa", 2 * fbuf + 1), ("ft", 3 * fbuf), ("ft", 3 * fbuf + 1), ("ft", 3 * fbuf + 2)
                            for gv, (ps, pk) in enumerate([(pg, kg), (pv, kv_)]):
                                for k in range(NCH):
                                    mm(ps[:, 0:sl + 2], upw[:, k, gv, pj * 128:(pj + 1) * 128], hn[:, k, c0 - 2:c0 + sl],
                                       k == 0, k == NCH - 1, wk + hsub[si][k], [pk])
                            for gv, (ps, pk, A, ak) in enumerate([(pg, kg, Ag, kA), (pv, kv_, Av, kV)]):
                                fc = gv * NFC + j
                                w0 = vcol("cw", cwb + 0 * 44 + fc)
                                w1 = vcol("cw", cwb + 1 * 44 + fc)
                                w2 = vcol("cw", cwb + 2 * 44 + fc)
                                bb = vcol("cb", l * 44 + fc)
                                act(A[:, 0:sl], ps[:, 2:sl + 2], AF.Identity, [pk, "vecs"], [ak], bias=bb, scale=w2)
                                stt("dve", A[:, 0:sl], ps[:, 1:sl + 1], w1, A[:, 0:sl], ALU.mult, ALU.add, [pk, ak, "vecs"], [ak])
                                stt("dve", A[:, 0:sl], ps[:, 0:sl], w0, A[:, 0:sl], ALU.mult, ALU.add, [pk, ak, "vecs"], [ak])
                            act(Sg[:, 0:sl], Ag[:, 0:sl], AF.Silu, [kA], [kS])
                            tt("pool", G[:, jj, s0 - t0:s0 - t0 + sl], Sg[:, 0:sl], Av[:, 0:sl], ALU.mult, [kS, kV], [("G", jj, si)])
                            fbuf ^= 1
                bbanks = [(pb[0], ("pb", 0)), (pb[1], ("pb", 1)), (pa[2], ("pa", 2)), (pa[3], ("pa", 3))]
                bctr = 0
                if half == 0:
                    inj = list(inject) if inject is not None else []
                    ngroups = 16 * len(subs) - 4
                    per = (len(inj) + ngroups - 1) // ngroups if inj else 0
                for dq in range(4):
                    w, wk = acquire(("down", l, half, dq))
                    dw = w[:, 0:npair * 256].rearrange("p (f d) -> p f d", f=npair)
                    for dl in range(2):
                        dc = dq * 2 + dl
                        for (s0, sl) in subs:
                            si = 0 if s0 < t0 + SUB else 1
                            ps, pk = bbanks[bctr % 4]
                            bctr += 1
                            for f in range(npair):
                                mm(ps[:, 0:sl], dw[:, f, dl * 128:(dl + 1) * 128], G[:, f, s0 - t0:s0 - t0 + sl],
                                   f == 0, f == npair - 1, wk + [("G", f, si)], [pk])
                            if bctr % 2 == 0:
                                tt("dve", x_sb[:, dc, s0:s0 + sl], ps[:, 0:sl], x_sb[:, dc, s0:s0 + sl], ALU.add,
                                   [pk] + xkeys([dc], s0, sl), xkeys([dc], s0, sl))
                            else:
                                eb = (bctr // 2) % 2
                                et, ek = [(ftmp[2], ("ft", 2)), (ftmp[5], ("ft", 5))][eb]
                                act(et[:, 0:sl], ps[:, 0:sl], AF.Copy, [pk], [ek])
                                tt("pool", x_sb[:, dc, s0:s0 + sl], et[:, 0:sl], x_sb[:, dc, s0:s0 + sl], ALU.add,
                                   [ek] + xkeys([dc], s0, sl), xkeys([dc], s0, sl))
                            for _ in range(per):
                                if inj:
                                    inj.pop(0)()
                if half == 1:
                    while inj:
                        inj.pop(0)()
                pj_base += npair
            zero_halo(t0, subs)

        def kv_stage(t0, subs):
            for (s0, sl) in subs:
                norm(s0, sl, t0, "kvg", 0)
            w, wk = acquire(("wkvK",))
            kw = w[:, 0:4096].rearrange("p (k g h c) -> p k g h c", k=8, g=2, h=2)
            for (s0, sl) in subs:
                si = 0 if s0 < t0 + SUB else 1
                c0 = HOFF + (s0 - t0)
                hk = [("hn", c, si) for c in ALLC]
                for g in range(2):
                    for hh in range(2):
                        for k in range(NCH):
                            mm(pb[hh][:, 0:sl], kw[:, k, g, hh, :], hn[:, k, c0:c0 + sl], k == 0, k == NCH - 1, wk + hk, [("pb", hh)])
                        act(KT[:, hh, g, s0:s0 + sl], pb[hh][:, 0:sl], AF.Identity, [("pb", hh), "vecs"],
                            [("KT", s0 // 128 + i) for i in range(sl // 128)], bias=vcol("bk", g * 2 + hh))
            w2, wk2 = acquire(("wkvV",))
            vw = w2[:, 0:1024].rearrange("p (k c) -> p k c", k=8)
            for (s0, sl) in subs:
                si = 0 if s0 < t0 + SUB else 1
                c0 = HOFF + (s0 - t0)
                hk = [("hn", c, si) for c in ALLC]
                for bi in range(sl // 128):
                    blk = s0 // 128 + bi
                    for k in range(NCH):
                        mm(pn[:, 0:128], hn[:, k, c0 + bi * 128:c0 + (bi + 1) * 128], vw[:, k, :], k == 0, k == NCH - 1, wk2 + hk, ["pn"])
                    tt("dve", Vaug[:, blk, :, 0:64], pn[:, 0:128].rearrange("p (g d) -> p g d", g=2),
                       reps[:, rl["bv"]:rl["bv"] + 128].rearrange("p (g d) -> p g d", g=2), ALU.add, ["pn", "reps", "vones"], [("V", blk)])

        OAK = [("pa", 2), ("pa", 3), "pn"]

        def oaug_ap(h):
            bank = [pa[2], pa[3], pn][h // 7]
            o = (h % 7) * 65
            return bank[:, o:o + 65], OAK[h // 7]

        def attn_pre(l, t0, subs):
            todo = []
            for (s0, sl) in subs:
                todo.extend(norm(s0, sl, t0, "n1g", l * 8, defer=True, dst=hn1, dkey="h1", pad=PAD))
            return todo

        def attn_stage(l, t0, subs):
            j = l - N_A
            QT = lambda c, a, b: G[:, c, a:b]
            OT = lambda c, a, b: G[:, c, SUB + a:SUB + b]
            for (s0, sl) in subs:
                si = 0 if s0 < t0 + SUB else 1
                c0 = HOFF + (s0 - t0)
                hk = [("h1", c, si) for c in ALLC]
                for hf in range(2):
                    w, wk = acquire(("wq", j, hf))
                    wv_ = w[:, 0:4096].rearrange("p (k c) -> p k c", k=8)
                    for cl in range(4):
                        c = hf * 4 + cl
                        ps, pk = pb[c % 2], ("pb", c % 2)
                        for k in range(NCH):
                            mm(ps[:, 0:sl], wv_[:, k, cl * 128:(cl + 1) * 128], hn1[:, k, c0:c0 + sl], k == 0, k == NCH - 1, wk + hk, [pk])
                        act(QT(c, 0, sl), ps[:, 0:sl], AF.Identity, [pk, "vecs"], [("G", c, 0)], bias=vcol("bq", j * 8 + c))
                nqb = sl // 128
                stbanks = [(pa[0], ("pa", 0)), (pa[1], ("pa", 1)), (pb[0], ("pb", 0)), (pb[1], ("pb", 1))]

                def qk(qb, hp):
                    n = s0 // 128 + qb
                    mview = masks[:, 2:4, :] if n == 2 else masks[:, 0:2, :]
                    g = hp // 4
                    stp, sk = stbanks[hp % 4]
                    for hh in range(2):
                        for kb in range(2):
                            kblk = n - 1 + kb
                            mm(stp[:, (hh * 2 + kb) * 128:(hh * 2 + kb + 1) * 128],
                               KT[:, hh, g, kblk * 128:(kblk + 1) * 128],
                               QT(hp, qb * 128, (qb + 1) * 128), True, True,
                               [("KT", kblk), ("G", hp, 0)], [sk])
                    pbuf = hp % NPT
                    Pk = ("Pt", pbuf)
                    act(Pt[pbuf], stp[:], AF.Exp, [sk], [Pk], scale=0.125)
                    pv4 = Pt[pbuf].rearrange("p (h k q) -> p h k q", h=2, k=2)
                    tt("dve", pv4, pv4, mview.unsqueeze(1).to_broadcast([128, 2, 2, 128]), ALU.mult, [Pk, "masks"], [Pk])

                def pvm(qb, hp):
                    n = s0 // 128 + qb
                    g = hp // 4
                    pbuf = hp % NPT
                    Pk = ("Pt", pbuf)
                    for hh in range(2):
                        h = hp * 2 + hh
                        oa, oak = oaug_ap(h)
                        for kb in range(2):
                            kblk = n - 1 + kb
                            mm(oa, Pt[pbuf][:, (hh * 2 + kb) * 128:(hh * 2 + kb + 1) * 128], Vaug[:, kblk, g, :],
                               kb == 0, kb == 1, [Pk, ("V", kblk), "vones"], [oak])

                def evac(qb, bi):
                    obuf = qb % 2
                    O = Osb[obuf]
                    h0, nh = [(0, 7), (7, 7), (14, 2)][bi]
                    bank = [pa[2], pa[3], pn][bi]
                    bv = bank[:, 0:nh * 65].rearrange("p (h d) -> p h d", h=nh)
                    dk, rk = ("den", bi), ("rden", bi)
                    tt("dve", den[:, h0:h0 + nh], bv[:, :, 64], esink[:, j * 16 + h0:j * 16 + h0 + nh], ALU.add,
                       [OAK[bi], "esink"], [dk])
                    P.add("dve", lambda e: e.reciprocal(rden[:, h0:h0 + nh], den[:, h0:h0 + nh]), [dk], [rk])
                    tt("dve", O[:, h0:h0 + nh, :], bv[:, :, 0:64], rden[:, h0:h0 + nh].unsqueeze(2).to_broadcast([128, nh, 64]),
                       ALU.mult, [OAK[bi], rk], [("Osb", obuf, bi)])

                def epilogue_pe(qb):
                    obuf = qb % 2
                    Of = Osb[obuf].rearrange("p h d -> p (h d)")
                    oks = [("Osb", obuf, bi) for bi in range(3)]
                    for c in range(NCH):
                        tr(pt[:, c * 128:(c + 1) * 128], Of[:, c * 128:(c + 1) * 128], oks, ["pt"])
                    act(G[:, 0:8, SUB + qb * 128:SUB + (qb + 1) * 128], pt[:].rearrange("p (c q) -> p c q", c=8), AF.Copy,
                        ["pt"], [("G", c, 1) for c in ALLC])

                units = [(qb, hp) for qb in range(nqb) for hp in range(8)]
                LA = 3
                for u in range(LA):
                    qk(*units[u])
                for u, (qb, hp) in enumerate(units):
                    if u + LA < len(units):
                        qk(*units[u + LA])
                    pvm(qb, hp)
                    if hp == 3:
                        evac(qb, 0)
                    if hp == 6:
                        evac(qb, 1)
                    if hp == 7:
                        evac(qb, 2)
                    if hp == 2 and qb > 0:
                        epilogue_pe(qb - 1)
                epilogue_pe(nqb - 1)
                o0 = max(s0, ATTN_FFN_START) - s0
                ol = sl - o0
                for hf in range(2):
                    w, wk = acquire(("wo", j, hf))
                    wv_ = w[:, 0:4096].rearrange("p (k c) -> p k c", k=8)
                    for dl in range(4):
                        dc = hf * 4 + dl
                        ps, pk = pb[dc % 2], ("pb", dc % 2)
                        for c in range(NCH):
                            mm(ps[:, 0:ol], wv_[:, c, dl * 128:(dl + 1) * 128], OT(c, o0, sl), c == 0, c == NCH - 1,
                               wk + [("G", c, 1)], [pk])
                        stt("dve", x_sb[:, dc, s0 + o0:s0 + sl], ps[:, 0:ol], vcol("bo", j * 8 + dc), x_sb[:, dc, s0 + o0:s0 + sl],
                            ALU.add, ALU.add, [pk, "vecs"] + xkeys([dc], s0, sl), xkeys([dc], s0, sl))
            zero_halo(t0, subs)

        def dump_x():
            dv = dbg.rearrange("(c p) t -> p c t", p=128)
            for c in range(NCH):
                dma("sp", dv[:, c, :], x_sb[:, c, :], "out", xkeys([c], 0, T), [("y", c)])
            P.add("sp", lambda e: e.nop(), [("y", c) for c in ALLC], ())

        stopped = False
        pre_done = set()
        fin = {"nout": 0, "oi": 0}

        def final_norm(t0):
            yv = yT.rearrange("(c p) t -> p c t", p=128)
            obufs = [(pA[:, 0:SUB], "pA"), (pA[:, SUB:2 * SUB], "pA2"), (pB[:, 0:SUB], "pB"), (pB[:, SUB:2 * SUB], "pB2")]
            for (s0, sl) in [(t0, SUB), (t0 + SUB, SUB)]:
                if s0 + sl <= HALO:
                    continue
                if s0 < HALO:
                    s0, sl = HALO, s0 + sl - HALO
                act(sq[:, 0:4, 0:sl], x_sb[:, 0:4, s0:s0 + sl], AF.Square, xkeys(range(0, 4), s0, sl), [("sq", 0)])
                act(sq[:, 4:8, 0:sl], x_sb[:, 4:8, s0:s0 + sl], AF.Square, xkeys(range(4, 8), s0, sl), [("sq", 1)])
                for c in range(NCH):
                    mm(pn[:, 0:sl], ones[:], sq[:, c, 0:sl], c == 0, c == NCH - 1, [("sq", c // 4), "ones"], ["pn"])
                rsqrt_pn(sl, 0)
                for c in range(NCH):
                    ob, okk = obufs[fin["oi"] % 4]
                    semk = f"out{fin['oi'] % 4}"
                    fin["oi"] += 1
                    stt("dve", ob[:, 0:sl], x_sb[:, c, s0:s0 + sl], vcol("fing", c), rstd2[:, 0, 0:sl], ALU.mult, ALU.mult,
                        xkeys([c], s0, sl) + [("rstd", 0), "vecs"], [okk])
                    dma("sp", yv[:, c, s0 - HALO:s0 - HALO + sl], ob[:, 0:sl], semk, [okk], [("y", fin["nout"])])
                    fin["nout"] += 1

        def pre_of(l, mi):
            attn = l >= N_A
            t0, subs = macro_tiles("attn" if attn else "pool")[mi]
            return attn_pre(l, t0, subs) if attn else pool_pre(l, t0, subs)

        for l in range(DEPTH):
            attn = l >= N_A
            tiles = macro_tiles("attn" if attn else "pool")
            for mi, (t0, subs) in enumerate(tiles):
                fsubs = macro_tiles("attn_ffn")[mi][1] if attn else subs
                if (l, mi) not in pre_done:
                    for f in pre_of(l, mi):
                        f()
                if attn:
                    attn_stage(l, t0, subs)
                else:
                    pool_post(l, t0, subs)
                if stop_here(l, "M"):
                    continue
                inject = None
                nxt = (l, mi + 1) if mi + 1 < len(tiles) else (l + 1, 0)
                if nxt[0] < DEPTH and not (stop_after is not None and stop_after[0] == l and nxt[0] != l):
                    inject = pre_of(*nxt)
                    pre_done.add(nxt)
                ffn_stage(l, t0, fsubs, mi == 0, inject)
                if l == N_A - 1:
                    kv_stage(t0, macro_tiles("std")[mi][1])
                if l == DEPTH - 1 and stop_after is None:
                    final_norm(t0)
            if stop_after is not None and stop_after[0] == l:
                dump_x()
                stopped = True
                break

        if not stopped:
            P.add("sp", lambda e: e.nop(), [("y", i) for i in range(fin["nout"])], ())

        assert ring["cur"] == len(ring["plan"]), (ring["cur"], len(ring["plan"]))
        sems = {e: st.enter_context(nc.semaphore("s_" + e)) for e in ENG_NAMES}
        dma_sems = {k: st.enter_context(nc.semaphore("d_" + k)) for k in P.dma_keys()}
        block = st.enter_context(nc.Block())
        P.emit(sems, dma_sems, block)
    return nc


_NC_CACHE = {}


def kernel(**inputs):
    in_maps = host_prep(inputs)
    if "full" not in _NC_CACHE:
        _NC_CACHE["full"] = build_program(None)
    nc = _NC_CACHE["full"]
    res = run_bass_kernel_spmd(nc, in_maps, core_ids=list(range(NCORES)))
    out = np.empty((2, SEQ, D), np.float32)
    for core in range(NCORES):
        b = core // 4
        t0 = (core % 4) * TOWN
        out[b, t0:t0 + TOWN, :] = np.asarray(res.results[core]["yT"]).T
    return out
```

```python
import numpy as np
from contextlib import ExitStack
import concourse.bass as bass
import concourse.mybir as mybir
from concourse.bass_utils import run_bass_kernel_spmd

F32 = mybir.dt.float32
BF16 = mybir.dt.bfloat16
AF = mybir.ActivationFunctionType
ALU = mybir.AluOpType

NCORES = 8
D = 1024
NCH = 8
SEQ = 8192
TOWN = 2048
HALO = 256
T = TOWN + HALO
NBLK = T // 128
DFF = 2816
NFC = 22
DEPTH = 4
N_A = 2
EPS = 1e-5
HOFF = 16
MACRO = 768
SUB = 384
NSLOT = 3
PAD = 3
SLOT_ELEMS = 4096
MAXPARTS = 5
HALF_PAIRS = (12, 10)

ENG_NAMES = ["pe", "act", "dve", "pool", "sp"]


class Op:
    __slots__ = ("eng", "fn", "is_dma", "semkey", "waits", "signal", "count", "idx", "group")


class Prog:
    def __init__(self):
        self.ops = []
        self.last_writer = {}
        self.readers = {}

    def add(self, eng, fn, reads=(), writes=(), dma=None, group=None):
        op = Op()
        op.group = group
        op.eng = eng
        op.fn = fn
        op.is_dma = dma is not None
        op.semkey = dma
        op.signal = False
        op.count = None
        op.idx = len(self.ops)
        deps = set()
        for k in reads:
            w = self.last_writer.get(k)
            if w is not None:
                deps.add(w)
        for k in writes:
            w = self.last_writer.get(k)
            if w is not None:
                deps.add(w)
            for r in self.readers.get(k, ()):
                deps.add(r)
        if eng == "pe" and dma is None:
            deps = {d for d in deps if not (self.ops[d].eng == "pe" and not self.ops[d].is_dma)}
        op.waits = deps
        for k in reads:
            self.readers.setdefault(k, []).append(op.idx)
        for k in writes:
            self.last_writer[k] = op.idx
            self.readers[k] = []
        self.ops.append(op)
        return op

    def dma_keys(self):
        return sorted({op.semkey for op in self.ops if op.is_dma})

    def emit(self, sems, dma_sems, block):
        ops = self.ops
        for op in ops:
            for d in op.waits:
                ops[d].signal = True
        cnt = {}
        for op in ops:
            key = ("dma", op.semkey) if op.is_dma else ("eng", op.eng)
            if op.is_dma or op.signal:
                cnt[key] = cnt.get(key, 0) + (16 if op.is_dma else 1)
                op.count = (key, cnt[key])
        gmax = {}
        for op in ops:
            if op.is_dma and op.group is not None:
                gk = (op.semkey, op.group)
                gmax[gk] = max(gmax.get(gk, 0), op.count[1])
        for op in ops:
            if op.is_dma and op.group is not None:
                op.count = (op.count[0], gmax[(op.semkey, op.group)])
        per_eng = {e: [] for e in ENG_NAMES}
        for op in ops:
            per_eng[op.eng].append(op)

        def sem_of(key):
            return dma_sems[key[1]] if key[0] == "dma" else sems[key[1]]

        def run(engname, e):
            seen = {}
            for op in per_eng[engname]:
                need = {}
                for d in op.waits:
                    key, v = ops[d].count
                    if need.get(key, 0) < v:
                        need[key] = v
                for key, v in need.items():
                    if seen.get(key, 0) >= v:
                        continue
                    seen[key] = v
                    e.wait_ge(sem_of(key), v)
                ins = op.fn(e)
                if op.count is not None:
                    ins.then_inc(sem_of(op.count[0]), 16 if op.is_dma else 1)

        @block.tensor
        def _(e):
            run("pe", e)

        @block.scalar
        def _(e):
            run("act", e)

        @block.vector
        def _(e):
            run("dve", e)

        @block.gpsimd
        def _(e):
            run("pool", e)

        @block.sync
        def _(e):
            run("sp", e)


def vec_layout():
    lay = {}
    off = 0
    for name, n in [("n1g", 32), ("n2g", 32), ("psc", 16), ("kvg", 8), ("fing", 8), ("bq", 16),
                    ("bo", 16), ("cw", 4 * 3 * 44), ("cb", 4 * 44), ("bk", 4), ("flag", 1)]:
        lay[name] = off
        off += n
    return lay, off


def rep_layout():
    lay = {}
    off = 0
    for name, n in [("bv", 128), ("sink", 32), ("invcnt", 128)]:
        lay[name] = off
        off += n
    return lay, off


def _cols(a):
    a = np.asarray(a, dtype=np.float32)
    lead = int(np.prod(a.shape[:-1])) if a.ndim > 1 else 1
    n = a.shape[-1] // 128
    return np.ascontiguousarray(a.reshape(lead, n, 128).transpose(2, 0, 1).reshape(128, lead * n))


def host_prep(inputs):
    x = np.asarray(inputs["x"], dtype=np.float32)
    vl, nv = vec_layout()
    rl, nr = rep_layout()
    common_vec = np.zeros((128, nv), np.float32)

    def put(name, arr):
        common_vec[:, vl[name]:vl[name] + arr.shape[1]] = arr

    put("n1g", _cols(inputs["norm1_g"]))
    put("n2g", _cols(inputs["norm2_g"]))
    put("psc", _cols(inputs["pool_scale"]))
    put("kvg", _cols(inputs["kv_norm_g"]))
    put("fing", _cols(inputs["final_g"]))
    put("bq", _cols(inputs["b_q"]))
    put("bo", _cols(inputs["b_o"]))
    put("cw", _cols(np.asarray(inputs["ffn_conv_w"]).reshape(12, 5632)))
    put("cb", _cols(inputs["ffn_conv_b"]))
    bkv = np.asarray(inputs["b_kv"], np.float32)
    z64 = np.zeros(64, np.float32)
    bk = np.stack([np.concatenate([bkv[0:64], z64]), np.concatenate([z64, bkv[0:64]]),
                   np.concatenate([bkv[64:128], z64]), np.concatenate([z64, bkv[64:128]])], axis=1)
    put("bk", bk)

    s_idx = np.arange(128)[:, None]
    q_idx = np.arange(128)[None, :]
    m_prev = (s_idx > q_idx).astype(np.float32)
    m_diag = (s_idx <= q_idx).astype(np.float32)
    ident = np.eye(128, dtype=np.float32)

    weights = {k: np.ascontiguousarray(np.asarray(inputs[k], dtype=np.float32))
               for k in ["pool_w", "w_kv", "w_q", "w_o", "ffn_up", "ffn_down"]}
    wkv = weights["w_kv"]
    wk_pad = np.zeros((D, 2, 2, 128), np.float32)
    for g in range(2):
        wk_pad[:, g, 0, 0:64] = wkv[:, g * 64:(g + 1) * 64]
        wk_pad[:, g, 1, 64:128] = wkv[:, g * 64:(g + 1) * 64]
    weights["wk_pad"] = wk_pad.reshape(D, 512)
    in_maps = []
    for core in range(NCORES):
        b = core // 4
        t0 = (core % 4) * TOWN
        first = (core % 4) == 0
        xt = np.zeros((D, T), np.float32)
        if first:
            xt[:, HALO:] = x[b, 0:TOWN, :].T
        else:
            xt[:, :] = x[b, t0 - HALO:t0 + TOWN, :].T
        vec = common_vec.copy()
        vec[:, vl["flag"]] = 0.0 if first else 1.0
        rep = np.zeros((128, nr), np.float32)
        rep[:, rl["bv"]:rl["bv"] + 128] = bkv[128:256][None, :]
        rep[:, rl["sink"]:rl["sink"] + 32] = np.asarray(inputs["sinks"], np.float32).reshape(1, 32)
        inv = np.zeros((8, 16), np.float32)
        for c in range(8):
            win = 2 ** (c // 2 + 1)
            for tt in range(16):
                cnt = min(tt + 1, win) if first else win
                inv[c, tt] = 1.0 / cnt
        rep[:, rl["invcnt"]:rl["invcnt"] + 128] = inv.reshape(1, 128)
        masks = np.stack([m_prev, m_diag, m_prev * (0.0 if first else 1.0), m_diag], axis=1)
        m = {"xT": xt, "vecs": vec, "reps": rep, "masks": np.ascontiguousarray(masks), "ident": ident}
        m.update(weights)
        in_maps.append(m)
    return in_maps


POOL_START = 64
ATTN_FFN_START = 248


def macro_tiles(kind):
    res = []
    for m in range(T // MACRO):
        t0 = m * MACRO
        subs = [(t0, SUB), (t0 + SUB, SUB)]
        if m == 0:
            first = {"pool": POOL_START, "attn": 128, "attn_ffn": ATTN_FFN_START, "std": 0}[kind]
            subs = [(first, SUB - first), (SUB, SUB)]
        res.append((t0, subs))
    return res


def build_program(stop_after=None):
    nc = bass.Bass("TRN2", target_bir_lowering=False)
    vl, nv = vec_layout()
    rl, nr = rep_layout()
    dr = {}
    dr["xT"] = nc.dram_tensor("xT", [D, T], F32, kind="ExternalInput").ap()
    dr["vecs"] = nc.dram_tensor("vecs", [128, nv], F32, kind="ExternalInput").ap()
    dr["reps"] = nc.dram_tensor("reps", [128, nr], F32, kind="ExternalInput").ap()
    dr["masks"] = nc.dram_tensor("masks", [128, 4, 128], F32, kind="ExternalInput").ap()
    dr["ident"] = nc.dram_tensor("ident", [128, 128], F32, kind="ExternalInput").ap()
    dr["pool_w"] = nc.dram_tensor("pool_w", [2, 4, 256, 256], F32, kind="ExternalInput").ap()
    dr["w_kv"] = nc.dram_tensor("w_kv", [1024, 256], F32, kind="ExternalInput").ap()
    dr["wk_pad"] = nc.dram_tensor("wk_pad", [1024, 512], F32, kind="ExternalInput").ap()
    dr["w_q"] = nc.dram_tensor("w_q", [2, 1024, 1024], F32, kind="ExternalInput").ap()
    dr["w_o"] = nc.dram_tensor("w_o", [2, 1024, 1024], F32, kind="ExternalInput").ap()
    dr["ffn_up"] = nc.dram_tensor("ffn_up", [4, 1024, 5632], F32, kind="ExternalInput").ap()
    dr["ffn_down"] = nc.dram_tensor("ffn_down", [4, 2816, 1024], F32, kind="ExternalInput").ap()
    if stop_after is None:
        yT = nc.dram_tensor("yT", [D, TOWN], F32, kind="ExternalOutput").ap()
    else:
        dbg = nc.dram_tensor("dbg", [D, T], F32, kind="ExternalOutput").ap()

    P = Prog()
    with ExitStack() as st:
        def sb(name, shape, dt):
            return st.enter_context(nc.sbuf_tensor(name, shape, dt))

        def psb(name, shape, dt):
            return st.enter_context(nc.psum_tensor(name, shape, dt))

        x_sb = sb("x_sb", [128, NCH, T], F32)
        hn = sb("hn", [128, NCH, HOFF + MACRO], BF16)
        G = sb("G", [128, 12, MACRO], BF16)
        pooled = sb("pooled", [128, NCH, MACRO], BF16)
        hn1 = sb("hn1", [128, NCH, HOFF + MACRO], BF16)
        _pf = pooled[:].rearrange("p c t -> p (c t)")
        Pt = [_pf[:, i * 512:(i + 1) * 512] for i in range(4)]
        Osb = [_pf[:, 2048 + i * 1024:2048 + (i + 1) * 1024].rearrange("p (h d) -> p h d", h=16) for i in range(2)]
        wr = [sb(f"wr{i}", [128, SLOT_ELEMS], BF16) for i in range(NSLOT)]
        KT = sb("KT", [128, 2, 2, T], BF16)
        Vaug = sb("Vaug", [128, NBLK, 2, 65], BF16)
        ftmp = [sb(f"ftmp{i}", [128, SUB], F32) for i in range(6)]
        sq = sb("sq", [128, NCH, SUB], BF16)
        rstd2 = sb("rstd", [128, 2, SUB], F32)
        rtmp = sb("rtmp", [128, SUB], F32)
        epsc = sb("epsc", [128, 1], F32)
        pA = sb("pA", [128, HOFF + MACRO], F32)
        pB = sb("pB", [128, HOFF + MACRO], F32)
        ptmp = sb("ptmp", [128, 16], F32)
        carry1 = sb("carry1", [128, NCH, 16], BF16)
        carry2 = sb("carry2", [128, NCH, 2], BF16)
        NPT = 4
        den = sb("den", [128, 16], F32)
        rden = sb("rden", [128, 16], F32)
        esink = sb("esink", [128, 32], F32)
        vecs = sb("vecs_sb", [128, nv], F32)
        reps = sb("reps_sb", [128, nr], F32)
        masks = sb("masks_sb", [128, 4, 128], BF16)
        ident = sb("ident_sb", [128, 128], BF16)
        ones = sb("ones_sb", [128, 128], BF16)

        pa = [psb(f"pa{i}", [128, 512], F32) for i in range(4)]
        pb = [psb(f"pb{i}", [128, 512], F32) for i in range(2)]
        pn = psb("pn", [128, 512], F32)
        pt = psb("pt", [128, 1024], BF16)

        def vcol(name, i):
            o = vl[name] + i
            return vecs[:, o:o + 1]

        def mm(out, lhsT, rhs, start, stop, reads, writes):
            P.add("pe", lambda e: e.matmul(out, lhsT=lhsT, rhs=rhs, start=start, stop=stop), reads, writes)

        def tr(out, in_, reads, writes):
            P.add("pe", lambda e: e.transpose(out, in_, ident[:]), list(reads) + ["ident"], writes)

        def act(out, in_, func, reads, writes, bias=None, scale=None):
            kw = {}
            if bias is not None:
                kw["bias"] = bias
            if scale is not None:
                kw["scale"] = scale
            P.add("act", lambda e: e.activation(out=out, in_=in_, func=func, **kw), reads, writes)

        def ts(eng, out, in0, s1, s2, op0, op1, reads, writes):
            if op1 is None:
                P.add(eng, lambda e: e.tensor_scalar(out, in0, s1, None, op0), reads, writes)
            else:
                P.add(eng, lambda e: e.tensor_scalar(out, in0, s1, s2, op0, op1), reads, writes)

        def stt(eng, out, in0, scalar, in1, op0, op1, reads, writes):
            P.add(eng, lambda e: e.scalar_tensor_tensor(out, in0, scalar, in1, op0, op1), reads, writes)

        def tt(eng, out, in0, in1, op, reads, writes):
            P.add(eng, lambda e: e.tensor_tensor(out, in0, in1, op), reads, writes)

        def cp(eng, out, in_, reads, writes):
            P.add(eng, lambda e: e.tensor_copy(out, in_), reads, writes)

        def mset(eng, ap, val, writes):
            P.add(eng, lambda e: e.memset(ap, val), (), writes)

        def dma(q, out, in_, semkey, reads, writes, group=None):
            P.add(q, lambda e: e.dma_start(out=out, in_=in_), reads, writes, dma=semkey, group=group)

        def xkeys(cs, t0, ln):
            return [("x", c, b) for c in cs for b in range(t0 // 128, (t0 + ln + 127) // 128)]

        ALLC = list(range(NCH))

        ring = {"plan": [], "issued": 0, "cur": 0}

        def slot_keys(s):
            return [("ws", s, i) for i in range(MAXPARTS)]

        def issue_upto(k):
            while ring["issued"] < min(k, len(ring["plan"])):
                i = ring["issued"]
                s = i % NSLOT
                tag, parts = ring["plan"][i]
                for pi, (ovf, iv) in enumerate(parts):
                    wk = [("ws", s, pi)]
                    if pi == len(parts) - 1:
                        wk = [("ws", s, j) for j in range(pi, MAXPARTS)]
                    dma("pool", ovf(wr[s]), iv, f"ws{s}", (), wk, group=i)
                ring["issued"] += 1

        def acquire(tag):
            k = ring["cur"]
            assert ring["plan"][k][0] == tag, (ring["plan"][k][0], tag)
            issue_upto(k + NSLOT)
            ring["cur"] += 1
            s = k % NSLOT
            return wr[s], slot_keys(s)

        def spec_pool_w(l):
            return (("poolw", l), [(lambda w: w[:, 0:2048].rearrange("p (g k d) -> p g k d", g=4, k=2),
                                    dr["pool_w"][l].rearrange("g (k p) d -> p g k d", p=128))])

        def spec_up(l, pp):
            upv = dr["ffn_up"][l].rearrange("(k p) f -> p k f", p=128)
            parts = []
            for gv in range(2):
                c0 = gv * DFF + pp * 256
                parts.append(((lambda gv: lambda w: w[:, 0:4096].rearrange("p (k g c) -> p k g c", k=8, g=2)[:, :, gv, :])(gv),
                              upv[:, :, c0:c0 + 256]))
            return (("up", l, pp), parts)

        def spec_down(l, half, dq):
            f0 = 0 if half == 0 else HALF_PAIRS[0]
            nf = HALF_PAIRS[half]
            dv = dr["ffn_down"][l].rearrange("(f p) d -> p f d", p=128)
            return (("down", l, half, dq), [((lambda nf: lambda w: w[:, 0:nf * 256].rearrange("p (f d) -> p f d", f=nf))(nf),
                                             dv[:, f0:f0 + nf, dq * 256:(dq + 1) * 256])])

        def spec_wq(j, hf):
            v = dr["w_q"][j].rearrange("(k p) f -> p k f", p=128)
            return (("wq", j, hf), [(lambda w: w[:, 0:4096].rearrange("p (k c) -> p k c", k=8), v[:, :, hf * 512:(hf + 1) * 512])])

        def spec_wo(j, hf):
            v = dr["w_o"][j].rearrange("(k p) f -> p k f", p=128)
            return (("wo", j, hf), [(lambda w: w[:, 0:4096].rearrange("p (k c) -> p k c", k=8), v[:, :, hf * 512:(hf + 1) * 512])])

        def spec_wkv():
            v = dr["w_kv"].rearrange("(k p) f -> p k f", p=128)
            vk = dr["wk_pad"].rearrange("(k p) f -> p k f", p=128)
            return [(("wkvK",), [(lambda w: w[:, 0:4096].rearrange("p (k c) -> p k c", k=8), vk)]),
                    (("wkvV",), [(lambda w: w[:, 0:1024].rearrange("p (k c) -> p k c", k=8), v[:, :, 128:256])])]

        def stop_here(l, stage):
            return stop_after is not None and tuple(stop_after) == (l, stage)

        done = False
        for l in range(DEPTH):
            attn = l >= N_A
            for (t0, subs) in macro_tiles("attn" if attn else "pool"):
                if attn:
                    for _ in subs:
                        ring["plan"] += [spec_wq(l - N_A, 0), spec_wq(l - N_A, 1), spec_wo(l - N_A, 0), spec_wo(l - N_A, 1)]
                else:
                    ring["plan"].append(spec_pool_w(l))
                if not (stop_after is not None and tuple(stop_after) == (l, "M")):
                    pj = 0
                    for half in range(2):
                        for _ in range(HALF_PAIRS[half] // 2):
                            ring["plan"].append(spec_up(l, pj))
                            pj += 1
                        for dq in range(4):
                            ring["plan"].append(spec_down(l, half, dq))
                    if l == N_A - 1:
                        ring["plan"] += spec_wkv()
            if stop_after is not None and stop_after[0] == l:
                break

        xv = dr["xT"].rearrange("(c p) t -> p c t", p=128)
        for m in range(T // MACRO):
            for c in range(NCH):
                dma("sp", x_sb[:, c, m * MACRO:(m + 1) * MACRO], xv[:, c, m * MACRO:(m + 1) * MACRO], f"xl{m}",
                    (), xkeys([c], m * MACRO, MACRO), group=m)
            if m == 0:
                dma("sp", vecs[:], dr["vecs"], "cvec", (), ["vecs"])
                dma("sp", reps[:], dr["reps"], "crep", (), ["reps"])
        dma("pool", masks[:], dr["masks"], "cmask", (), ["masks"])
        dma("pool", ident[:], dr["ident"], "cident", (), ["ident"])
        issue_upto(NSLOT)
        mset("dve", ones[:], 1.0 / 1024.0, ["ones"])
        mset("pool", hn[:], 0.0, [("hn", c, s_) for c in ALLC for s_ in range(2)] + [("hnh", c) for c in ALLC])
        mset("pool", hn1[:], 0.0, [("h1", c, s_) for c in ALLC for s_ in range(2)] + [("h1h", c) for c in ALLC])
        mset("dve", epsc[:], EPS, ["epsc"])
        mset("dve", Vaug[:, :, :, 64:65], 1.0, ["vones"])
        act(esink[:], reps[:, rl["sink"]:rl["sink"] + 32], AF.Exp, ["reps"], ["esink"])

        def rsqrt_pn(sl, ri=0):
            rstd = rstd2[:, ri, :]
            rk = ("rstd", ri)
            act(rtmp[:, 0:sl], pn[:, 0:sl], AF.Ln, ["pn", "epsc"], ["rtmp"], bias=epsc[:, 0:1])
            act(rstd[:, 0:sl], rtmp[:, 0:sl], AF.Exp, ["rtmp"], [rk], scale=-0.5)

        def norm_stt(c, s0, sl, col0, si, dst, dkey, gname, gidx0):
            return lambda: stt("dve", dst[:, c, col0:col0 + sl], x_sb[:, c, s0:s0 + sl], vcol(gname, gidx0 + c), rstd2[:, si, 0:sl],
                               ALU.mult, ALU.mult, xkeys([c], s0, sl) + [("rstd", si), "vecs"], [(dkey, c, si)])

        def norm(s0, sl, t0, gname, gidx0, defer=False, dst=None, dkey="hn", pad=0, stt_later=False):
            dst = hn if dst is None else dst
            si = 0 if s0 < t0 + SUB else 1
            col0 = HOFF + (s0 - t0)
            todo = []
            D_ = todo.append
            D_(lambda: act(sq[:, 0:4, 0:sl], x_sb[:, 0:4, s0:s0 + sl], AF.Square, xkeys(range(0, 4), s0, sl), [("sq", 0)]))
            D_(lambda: act(sq[:, 4:8, 0:sl], x_sb[:, 4:8, s0:s0 + sl], AF.Square, xkeys(range(4, 8), s0, sl), [("sq", 1)]))
            for _ in range(pad):
                D_(lambda: None)

            def mms():
                for c in range(NCH):
                    mm(pn[:, 0:sl], ones[:], sq[:, c, 0:sl], c == 0, c == NCH - 1, [("sq", c // 4), "ones"], ["pn"])
            D_(mms)
            D_(lambda: rsqrt_pn(sl, si))
            if stt_later:
                return todo
            for c in range(NCH):
                D_(norm_stt(c, s0, sl, col0, si, dst, dkey, gname, gidx0))
            if defer:
                return todo
            for f in todo:
                f()
            return si

        def zero_halo(t0, subs):
            if t0 != 0:
                return
            lo = subs[0][0]
            ts("dve", x_sb[:, :, lo:HALO], x_sb[:, :, lo:HALO], vcol("flag", 0), None, ALU.mult, None,
               xkeys(ALLC, lo, HALO - lo) + ["vecs"], xkeys(ALLC, lo, HALO - lo))

        def pool_pre(l, t0, subs):
            L = MACRO
            W = HOFF + L
            todo = []
            D_ = todo.append
            if t0 == 0:
                D_(lambda: mset("pool", hn1[:, :, 0:HOFF], 0.0, [("h1h", c) for c in ALLC]))
            else:
                D_(lambda: cp("pool", hn1[:, :, 0:HOFF], carry1[:], ["carry1"], [("h1h", c) for c in ALLC]))
            for (s0, sl) in subs:
                todo.extend(norm(s0, sl, t0, "n1g", l * 8, defer=True, dst=hn1, dkey="h1", pad=PAD, stt_later=True))
            for c in range(NCH):
                nl = c // 2 + 1
                win = 2 ** nl
                hk = [("h1", c, 0), ("h1", c, 1), ("h1h", c)]
                for (s0, sl) in subs:
                    si_ = 0 if s0 < t0 + SUB else 1
                    D_(norm_stt(c, s0, sl, HOFF + (s0 - t0), si_, hn1, "h1", "n1g", l * 8))
                src = hn1[:, c, :]
                bufs = [pA, pB]
                cur = None
                for lev in range(nl):
                    sh = 2 ** lev
                    lo = 2 * sh - 1
                    dst = bufs[lev % 2]
                    dk = "pA" if lev % 2 == 0 else "pB"
                    if lev == 0:
                        D_((lambda dst, lo, sh, hk, dk, src: lambda: tt("dve", dst[:, lo:W], src[:, lo:W], src[:, lo - sh:W - sh], ALU.add, hk, [dk]))(dst, lo, sh, hk, dk, src))
                    else:
                        ck = "pA" if (lev - 1) % 2 == 0 else "pB"
                        D_((lambda dst, lo, sh, ck, dk, cur: lambda: tt("dve", dst[:, lo:W], cur[:, lo:W], cur[:, lo - sh:W - sh], ALU.add, [ck], [dk]))(dst, lo, sh, ck, dk, cur))
                    cur = dst
                ck = "pA" if (nl - 1) % 2 == 0 else "pB"
                D_((lambda c, cur, win, src, ck, hk: lambda: stt("dve", pooled[:, c, 0:L], cur[:, HOFF:W], 1.0 / win, src[:, HOFF:W], ALU.mult, ALU.subtract,
                                                                 [ck] + hk, [("pl", c, 0), ("pl", c, 1)]))(c, cur, win, src, ck, hk))
                if t0 == 0:
                    io = rl["invcnt"] + c * 16
                    D_((lambda c, cur, io, ck: lambda: tt("dve", ptmp[:], cur[:, HOFF + HALO:HOFF + HALO + 16], reps[:, io:io + 16], ALU.mult,
                                                          [ck, "reps"], ["ptmp"]))(c, cur, io, ck))
                    D_((lambda c, src, hk: lambda: tt("dve", pooled[:, c, HALO:HALO + 16], ptmp[:], src[:, HOFF + HALO:HOFF + HALO + 16], ALU.subtract,
                                                      ["ptmp"] + hk, [("pl", c, 0)]))(c, src, hk))
            D_(lambda: cp("pool", carry1[:], hn1[:, :, HOFF + L - 16:HOFF + L], [("h1", c, 1) for c in ALLC], ["carry1"]))
            return todo

        def pool_post(l, t0, subs):
            w, wk = acquire(("poolw", l))
            pw = w[:, 0:2048].rearrange("p (g k d) -> p g k d", g=4, k=2)
            for (s0, sl) in subs:
                si = 0 if s0 < t0 + SUB else 1
                c0 = s0 - t0
                for dc in range(NCH):
                    g = dc // 2
                    ps = pb[dc % 2]
                    for kc in range(2):
                        mm(ps[:, 0:sl], pw[:, g, kc, (dc % 2) * 128:(dc % 2) * 128 + 128], pooled[:, 2 * g + kc, c0:c0 + sl],
                           kc == 0, kc == 1, wk + [("pl", 2 * g + kc, si)], [("pb", dc % 2)])
                    stt("dve", x_sb[:, dc, s0:s0 + sl], ps[:, 0:sl], vcol("psc", l * 8 + dc), x_sb[:, dc, s0:s0 + sl],
                        ALU.mult, ALU.add, [("pb", dc % 2), "vecs"] + xkeys([dc], s0, sl), xkeys([dc], s0, sl))
            zero_halo(t0, subs)

        def ffn_stage(l, t0, subs, first_tile, inject=None):
            L = MACRO
            if first_tile:
                mset("pool", hn[:, :, HOFF - 2:HOFF], 0.0, [("hnh", c) for c in ALLC])
            else:
                cp("pool", hn[:, :, HOFF - 2:HOFF], carry2[:], ["carry2"], [("hnh", c) for c in ALLC])
            norm(subs[0][0], subs[0][1], t0, "n2g", l * 8)
            late = norm(subs[1][0], subs[1][1], t0, "n2g", l * 8, defer=True)
            late.pop(0)()
            late.pop(0)()
            late.append(lambda: cp("pool", carry2[:], hn[:, :, HOFF + L - 2:HOFF + L], [("hn", c, 1) for c in ALLC], ["carry2"]))
            hsub = [[[("hn", c, 0), ("hnh", c)] for c in ALLC],
                    [[("hn", c, 1), ("hn", c, 0)] for c in ALLC]]
            cwb = l * 3 * 44
            pj_base = 0
            fbuf = 0
            for half in range(2):
                npair = HALF_PAIRS[half]
                for pp in range(npair // 2):
                    w, wk = acquire(("up", l, pj_base // 2 + pp))
                    upw = w[:, 0:4096].rearrange("p (k g c) -> p k g c", k=8, g=2)
                    for (s0, sl) in subs:
                        for pj in range(2):
                            j = pj_base + pp * 2 + pj
                            jj = pp * 2 + pj
                            si = 0 if s0 < t0 + SUB else 1
                            c0 = HOFF + (s0 - t0)
                            pg = pa[2 * fbuf]
                            pv = pa[2 * fbuf + 1]
                            Ag, Av, Sg = ftmp[3 * fbuf], ftmp[3 * fbuf + 1], ftmp[3 * fbuf + 2]
                            kg, kv_, kA, kV, kS = ("pa", 2 * fbuf), ("pa", 2 * fbuf + 1), ("ft", 3 * fbuf), ("ft", 3 * fbuf + 1), ("ft", 3 * fbuf + 2)
                            for gv, (ps, pk) in enumerate([(pg, kg), (pv, kv_)]):
                                for k in range(NCH):
                                    mm(ps[:, 0:sl + 2], upw[:, k, gv, pj * 128:(pj + 1) * 128], hn[:, k, c0 - 2:c0 + sl],
                                       k == 0, k == NCH - 1, wk + hsub[si][k], [pk])
                            while late:
                                late.pop(0)()
                            for gv, (ps, pk, A, ak) in enumerate([(pg, kg, Ag, kA), (pv, kv_, Av, kV)]):
                                fc = gv * NFC + j
                                w0 = vcol("cw", cwb + 0 * 44 + fc)
                                w1 = vcol("cw", cwb + 1 * 44 + fc)
                                w2 = vcol("cw", cwb + 2 * 44 + fc)
                                bb = vcol("cb", l * 44 + fc)
                                act(A[:, 0:sl], ps[:, 2:sl + 2], AF.Identity, [pk, "vecs"], [ak], bias=bb, scale=w2)
                                stt("dve", A[:, 0:sl], ps[:, 1:sl + 1], w1, A[:, 0:sl], ALU.mult, ALU.add, [pk, ak, "vecs"], [ak])
                                stt("dve", A[:, 0:sl], ps[:, 0:sl], w0, A[:, 0:sl], ALU.mult, ALU.add, [pk, ak, "vecs"], [ak])
                            act(Sg[:, 0:sl], Ag[:, 0:sl], AF.Silu, [kA], [kS])
                            tt("pool", G[:, jj, s0 - t0:s0 - t0 + sl], Sg[:, 0:sl], Av[:, 0:sl], ALU.mult, [kS, kV], [("G", jj, si)])
                            fbuf ^= 1
                bbanks = [(pb[0], ("pb", 0)), (pb[1], ("pb", 1)), (pa[2], ("pa", 2)), (pa[3], ("pa", 3))]
                bctr = 0
                if half == 0:
                    inj = list(inject) if inject is not None else []
                    ngroups = 16 * len(subs) - 4
                    per = (len(inj) + ngroups - 1) // ngroups if inj else 0
                for dq in range(4):
                    w, wk = acquire(("down", l, half, dq))
                    dw = w[:, 0:npair * 256].rearrange("p (f d) -> p f d", f=npair)
                    for dl in range(2):
                        dc = dq * 2 + dl
                        for (s0, sl) in subs:
                            si = 0 if s0 < t0 + SUB else 1
                            ps, pk = bbanks[bctr % 4]
                            bctr += 1
                            for f in range(npair):
                                mm(ps[:, 0:sl], dw[:, f, dl * 128:(dl + 1) * 128], G[:, f, s0 - t0:s0 - t0 + sl],
                                   f == 0, f == npair - 1, wk + [("G", f, si)], [pk])
                            tt("dve", x_sb[:, dc, s0:s0 + sl], ps[:, 0:sl], x_sb[:, dc, s0:s0 + sl], ALU.add,
                               [pk] + xkeys([dc], s0, sl), xkeys([dc], s0, sl))
                            for _ in range(per):
                                if inj:
                                    inj.pop(0)()
                if half == 1:
                    while inj:
                        inj.pop(0)()
                pj_base += npair
            zero_halo(t0, subs)

        def kv_stage(t0, subs):
            for (s0, sl) in subs:
                norm(s0, sl, t0, "kvg", 0)
            w, wk = acquire(("wkvK",))
            kw = w[:, 0:4096].rearrange("p (k g h c) -> p k g h c", k=8, g=2, h=2)
            for (s0, sl) in subs:
                si = 0 if s0 < t0 + SUB else 1
                c0 = HOFF + (s0 - t0)
                hk = [("hn", c, si) for c in ALLC]
                for g in range(2):
                    for hh in range(2):
                        for k in range(NCH):
                            mm(pb[hh][:, 0:sl], kw[:, k, g, hh, :], hn[:, k, c0:c0 + sl], k == 0, k == NCH - 1, wk + hk, [("pb", hh)])
                        act(KT[:, hh, g, s0:s0 + sl], pb[hh][:, 0:sl], AF.Identity, [("pb", hh), "vecs"],
                            [("KT", s0 // 128 + i) for i in range(sl // 128)], bias=vcol("bk", g * 2 + hh))
            w2, wk2 = acquire(("wkvV",))
            vw = w2[:, 0:1024].rearrange("p (k c) -> p k c", k=8)
            for (s0, sl) in subs:
                si = 0 if s0 < t0 + SUB else 1
                c0 = HOFF + (s0 - t0)
                hk = [("hn", c, si) for c in ALLC]
                for bi in range(sl // 128):
                    blk = s0 // 128 + bi
                    for k in range(NCH):
                        mm(pn[:, 0:128], hn[:, k, c0 + bi * 128:c0 + (bi + 1) * 128], vw[:, k, :], k == 0, k == NCH - 1, wk2 + hk, ["pn"])
                    tt("dve", Vaug[:, blk, :, 0:64], pn[:, 0:128].rearrange("p (g d) -> p g d", g=2),
                       reps[:, rl["bv"]:rl["bv"] + 128].rearrange("p (g d) -> p g d", g=2), ALU.add, ["pn", "reps", "vones"], [("V", blk)])

        OAK = [("pa", 2), ("pa", 3), "pn"]

        def oaug_ap(h):
            bank = [pa[2], pa[3], pn][h // 7]
            o = (h % 7) * 65
            return bank[:, o:o + 65], OAK[h // 7]

        def attn_pre(l, t0, subs):
            todo = []
            for (s0, sl) in subs:
                todo.extend(norm(s0, sl, t0, "n1g", l * 8, defer=True, dst=hn1, dkey="h1", pad=PAD))
            return todo

        def attn_stage(l, t0, subs):
            j = l - N_A
            QT = lambda c, a, b: G[:, c, a:b]
            OT = lambda c, a, b: G[:, c, SUB + a:SUB + b]
            for (s0, sl) in subs:
                si = 0 if s0 < t0 + SUB else 1
                c0 = HOFF + (s0 - t0)
                hk = [("h1", c, si) for c in ALLC]
                for hf in range(2):
                    w, wk = acquire(("wq", j, hf))
                    wv_ = w[:, 0:4096].rearrange("p (k c) -> p k c", k=8)
                    for cl in range(4):
                        c = hf * 4 + cl
                        ps, pk = pb[c % 2], ("pb", c % 2)
                        for k in range(NCH):
                            mm(ps[:, 0:sl], wv_[:, k, cl * 128:(cl + 1) * 128], hn1[:, k, c0:c0 + sl], k == 0, k == NCH - 1, wk + hk, [pk])
                        act(QT(c, 0, sl), ps[:, 0:sl], AF.Identity, [pk, "vecs"], [("G", c, 0)], bias=vcol("bq", j * 8 + c))
                nqb = sl // 128
                stbanks = [(pa[0], ("pa", 0)), (pa[1], ("pa", 1)), (pb[0], ("pb", 0)), (pb[1], ("pb", 1))]

                def qk(qb, hp):
                    n = s0 // 128 + qb
                    mview = masks[:, 2:4, :] if n == 2 else masks[:, 0:2, :]
                    g = hp // 4
                    stp, sk = stbanks[hp % 4]
                    for hh in range(2):
                        for kb in range(2):
                            kblk = n - 1 + kb
                            mm(stp[:, (hh * 2 + kb) * 128:(hh * 2 + kb + 1) * 128],
                               KT[:, hh, g, kblk * 128:(kblk + 1) * 128],
                               QT(hp, qb * 128, (qb + 1) * 128), True, True,
                               [("KT", kblk), ("G", hp, 0)], [sk])
                    pbuf = hp % NPT
                    Pk = ("Pt", pbuf)
                    act(Pt[pbuf], stp[:], AF.Exp, [sk], [Pk], scale=0.125)
                    pv4 = Pt[pbuf].rearrange("p (h k q) -> p h k q", h=2, k=2)
                    tt("dve", pv4, pv4, mview.unsqueeze(1).to_broadcast([128, 2, 2, 128]), ALU.mult, [Pk, "masks"], [Pk])

                def pvm(qb, hp):
                    n = s0 // 128 + qb
                    g = hp // 4
                    pbuf = hp % NPT
                    Pk = ("Pt", pbuf)
                    for hh in range(2):
                        h = hp * 2 + hh
                        oa, oak = oaug_ap(h)
                        for kb in range(2):
                            kblk = n - 1 + kb
                            mm(oa, Pt[pbuf][:, (hh * 2 + kb) * 128:(hh * 2 + kb + 1) * 128], Vaug[:, kblk, g, :],
                               kb == 0, kb == 1, [Pk, ("V", kblk), "vones"], [oak])

                def evac(qb, bi):
                    obuf = qb % 2
                    O = Osb[obuf]
                    h0, nh = [(0, 7), (7, 7), (14, 2)][bi]
                    bank = [pa[2], pa[3], pn][bi]
                    bv = bank[:, 0:nh * 65].rearrange("p (h d) -> p h d", h=nh)
                    dk, rk = ("den", bi), ("rden", bi)
                    tt("dve", den[:, h0:h0 + nh], bv[:, :, 64], esink[:, j * 16 + h0:j * 16 + h0 + nh], ALU.add,
                       [OAK[bi], "esink"], [dk])
                    P.add("dve", lambda e: e.reciprocal(rden[:, h0:h0 + nh], den[:, h0:h0 + nh]), [dk], [rk])
                    tt("dve", O[:, h0:h0 + nh, :], bv[:, :, 0:64], rden[:, h0:h0 + nh].unsqueeze(2).to_broadcast([128, nh, 64]),
                       ALU.mult, [OAK[bi], rk], [("Osb", obuf, bi)])

                def epilogue_pe(qb):
                    obuf = qb % 2
                    Of = Osb[obuf].rearrange("p h d -> p (h d)")
                    oks = [("Osb", obuf, bi) for bi in range(3)]
                    for c in range(NCH):
                        tr(pt[:, c * 128:(c + 1) * 128], Of[:, c * 128:(c + 1) * 128], oks, ["pt"])
                    act(G[:, 0:8, SUB + qb * 128:SUB + (qb + 1) * 128], pt[:].rearrange("p (c q) -> p c q", c=8), AF.Copy,
                        ["pt"], [("G", c, 1) for c in ALLC])

                units = [(qb, hp) for qb in range(nqb) for hp in range(8)]
                LA = 3
                for u in range(LA):
                    qk(*units[u])
                for u, (qb, hp) in enumerate(units):
                    if u + LA < len(units):
                        qk(*units[u + LA])
                    pvm(qb, hp)
                    if hp == 3:
                        evac(qb, 0)
                    if hp == 6:
                        evac(qb, 1)
                    if hp == 7:
                        evac(qb, 2)
                    if hp == 2 and qb > 0:
                        epilogue_pe(qb - 1)
                epilogue_pe(nqb - 1)
                o0 = max(s0, ATTN_FFN_START) - s0
                ol = sl - o0
                for hf in range(2):
                    w, wk = acquire(("wo", j, hf))
                    wv_ = w[:, 0:4096].rearrange("p (k c) -> p k c", k=8)
                    for dl in range(4):
                        dc = hf * 4 + dl
                        ps, pk = pb[dc % 2], ("pb", dc % 2)
                        for c in range(NCH):
                            mm(ps[:, 0:ol], wv_[:, c, dl * 128:(dl + 1) * 128], OT(c, o0, sl), c == 0, c == NCH - 1,
                               wk + [("G", c, 1)], [pk])
                        stt("dve", x_sb[:, dc, s0 + o0:s0 + sl], ps[:, 0:ol], vcol("bo", j * 8 + dc), x_sb[:, dc, s0 + o0:s0 + sl],
                            ALU.add, ALU.add, [pk, "vecs"] + xkeys([dc], s0, sl), xkeys([dc], s0, sl))
            zero_halo(t0, subs)

        def dump_x():
            dv = dbg.rearrange("(c p) t -> p c t", p=128)
            for c in range(NCH):
                dma("sp", dv[:, c, :], x_sb[:, c, :], "out", xkeys([c], 0, T), [("y", c)])
            P.add("sp", lambda e: e.nop(), [("y", c) for c in ALLC], ())

        stopped = False
        pre_done = set()
        fin = {"nout": 0, "oi": 0}

        def final_norm(t0):
            yv = yT.rearrange("(c p) t -> p c t", p=128)
            obufs = [(pA[:, 0:SUB], "pA"), (pA[:, SUB:2 * SUB], "pA2"), (pB[:, 0:SUB], "pB"), (pB[:, SUB:2 * SUB], "pB2")]
            for (s0, sl) in [(t0, SUB), (t0 + SUB, SUB)]:
                if s0 + sl <= HALO:
                    continue
                if s0 < HALO:
                    s0, sl = HALO, s0 + sl - HALO
                act(sq[:, 0:4, 0:sl], x_sb[:, 0:4, s0:s0 + sl], AF.Square, xkeys(range(0, 4), s0, sl), [("sq", 0)])
                act(sq[:, 4:8, 0:sl], x_sb[:, 4:8, s0:s0 + sl], AF.Square, xkeys(range(4, 8), s0, sl), [("sq", 1)])
                for c in range(NCH):
                    mm(pn[:, 0:sl], ones[:], sq[:, c, 0:sl], c == 0, c == NCH - 1, [("sq", c // 4), "ones"], ["pn"])
                rsqrt_pn(sl, 0)
                for c in range(NCH):
                    ob, okk = obufs[fin["oi"] % 4]
                    semk = f"out{fin['oi'] % 4}"
                    fin["oi"] += 1
                    stt("dve", ob[:, 0:sl], x_sb[:, c, s0:s0 + sl], vcol("fing", c), rstd2[:, 0, 0:sl], ALU.mult, ALU.mult,
                        xkeys([c], s0, sl) + [("rstd", 0), "vecs"], [okk])
                    dma("sp", yv[:, c, s0 - HALO:s0 - HALO + sl], ob[:, 0:sl], semk, [okk], [("y", fin["nout"])])
                    fin["nout"] += 1

        def pre_of(l, mi):
            attn = l >= N_A
            t0, subs = macro_tiles("attn" if attn else "pool")[mi]
            return attn_pre(l, t0, subs) if attn else pool_pre(l, t0, subs)

        for l in range(DEPTH):
            attn = l >= N_A
            tiles = macro_tiles("attn" if attn else "pool")
            for mi, (t0, subs) in enumerate(tiles):
                fsubs = macro_tiles("attn_ffn")[mi][1] if attn else subs
                if (l, mi) not in pre_done:
                    for f in pre_of(l, mi):
                        f()
                if attn:
                    attn_stage(l, t0, subs)
                else:
                    pool_post(l, t0, subs)
                if stop_here(l, "M"):
                    continue
                inject = None
                nxt = (l, mi + 1) if mi + 1 < len(tiles) else (l + 1, 0)
                if nxt[0] < DEPTH and not (stop_after is not None and stop_after[0] == l and nxt[0] != l):
                    inject = pre_of(*nxt)
                    pre_done.add(nxt)
                ffn_stage(l, t0, fsubs, mi == 0, inject)
                if l == N_A - 1:
                    kv_stage(t0, macro_tiles("std")[mi][1])
                if l == DEPTH - 1 and stop_after is None:
                    final_norm(t0)
            if stop_after is not None and stop_after[0] == l:
                dump_x()
                stopped = True
                break

        if not stopped:
            P.add("sp", lambda e: e.nop(), [("y", i) for i in range(fin["nout"])], ())

        assert ring["cur"] == len(ring["plan"]), (ring["cur"], len(ring["plan"]))
        sems = {e: st.enter_context(nc.semaphore("s_" + e)) for e in ENG_NAMES}
        dma_sems = {k: st.enter_context(nc.semaphore("d_" + k)) for k in P.dma_keys()}
        block = st.enter_context(nc.Block())
        P.emit(sems, dma_sems, block)
    return nc


_NC_CACHE = {}


def kernel(**inputs):
    in_maps = host_prep(inputs)
    if "full" not in _NC_CACHE:
        _NC_CACHE["full"] = build_program(None)
    nc = _NC_CACHE["full"]
    res = run_bass_kernel_spmd(nc, in_maps, core_ids=list(range(NCORES)))
    out = np.empty((2, SEQ, D), np.float32)
    for core in range(NCORES):
        b = core // 4
        t0 = (core % 4) * TOWN
        out[b, t0:t0 + TOWN, :] = np.asarray(res.results[core]["yT"]).T
    return out
```

```python
import numpy as np
from contextlib import ExitStack
import concourse.bass as bass
import concourse.mybir as mybir
from concourse.bass_utils import run_bass_kernel_spmd

F32 = mybir.dt.float32
BF16 = mybir.dt.bfloat16
AF = mybir.ActivationFunctionType
ALU = mybir.AluOpType

NCORES = 8
D = 1024
NCH = 8
SEQ = 8192
TOWN = 2048
HALO = 256
T = TOWN + HALO
NBLK = T // 128
DFF = 2816
NFC = 22
DEPTH = 4
N_A = 2
EPS = 1e-5
HOFF = 16
MACRO = 768
SUB = 384
NSLOT = 3
PAD = 3
SLOT_ELEMS = 4096
MAXPARTS = 5
HALF_PAIRS = (12, 10)

ENG_NAMES = ["pe", "act", "dve", "pool", "sp"]


class Op:
    __slots__ = ("eng", "fn", "is_dma", "semkey", "waits", "signal", "count", "idx", "group")


class Prog:
    def __init__(self):
        self.ops = []
        self.last_writer = {}
        self.readers = {}

    def add(self, eng, fn, reads=(), writes=(), dma=None, group=None):
        op = Op()
        op.group = group
        op.eng = eng
        op.fn = fn
        op.is_dma = dma is not None
        op.semkey = dma
        op.signal = False
        op.count = None
        op.idx = len(self.ops)
        deps = set()
        for k in reads:
            w = self.last_writer.get(k)
            if w is not None:
                deps.add(w)
        for k in writes:
            w = self.last_writer.get(k)
            if w is not None:
                deps.add(w)
            for r in self.readers.get(k, ()):
                deps.add(r)
        if eng == "pe" and dma is None:
            deps = {d for d in deps if not (self.ops[d].eng == "pe" and not self.ops[d].is_dma)}
        op.waits = deps
        for k in reads:
            self.readers.setdefault(k, []).append(op.idx)
        for k in writes:
            self.last_writer[k] = op.idx
            self.readers[k] = []
        self.ops.append(op)
        return op

    def dma_keys(self):
        return sorted({op.semkey for op in self.ops if op.is_dma})

    def emit(self, sems, dma_sems, block):
        ops = self.ops
        for op in ops:
            for d in op.waits:
                ops[d].signal = True
        cnt = {}
        for op in ops:
            key = ("dma", op.semkey) if op.is_dma else ("eng", op.eng)
            if op.is_dma or op.signal:
                cnt[key] = cnt.get(key, 0) + (16 if op.is_dma else 1)
                op.count = (key, cnt[key])
        gmax = {}
        for op in ops:
            if op.is_dma and op.group is not None:
                gk = (op.semkey, op.group)
                gmax[gk] = max(gmax.get(gk, 0), op.count[1])
        for op in ops:
            if op.is_dma and op.group is not None:
                op.count = (op.count[0], gmax[(op.semkey, op.group)])
        per_eng = {e: [] for e in ENG_NAMES}
        for op in ops:
            per_eng[op.eng].append(op)

        def sem_of(key):
            return dma_sems[key[1]] if key[0] == "dma" else sems[key[1]]

        def run(engname, e):
            seen = {}
            for op in per_eng[engname]:
                need = {}
                for d in op.waits:
                    key, v = ops[d].count
                    if need.get(key, 0) < v:
                        need[key] = v
                for key, v in need.items():
                    if seen.get(key, 0) >= v:
                        continue
                    seen[key] = v
                    e.wait_ge(sem_of(key), v)
                ins = op.fn(e)
                if op.count is not None:
                    ins.then_inc(sem_of(op.count[0]), 16 if op.is_dma else 1)

        @block.tensor
        def _(e):
            run("pe", e)

        @block.scalar
        def _(e):
            run("act", e)

        @block.vector
        def _(e):
            run("dve", e)

        @block.gpsimd
        def _(e):
            run("pool", e)

        @block.sync
        def _(e):
            run("sp", e)


def vec_layout():
    lay = {}
    off = 0
    for name, n in [("n1g", 32), ("n2g", 32), ("psc", 16), ("kvg", 8), ("fing", 8), ("bq", 16),
                    ("bo", 16), ("cw", 4 * 3 * 44), ("cb", 4 * 44), ("bk", 4), ("flag", 1)]:
        lay[name] = off
        off += n
    return lay, off


def rep_layout():
    lay = {}
    off = 0
    for name, n in [("bv", 128), ("sink", 32), ("invcnt", 128)]:
        lay[name] = off
        off += n
    return lay, off


def _cols(a):
    a = np.asarray(a, dtype=np.float32)
    lead = int(np.prod(a.shape[:-1])) if a.ndim > 1 else 1
    n = a.shape[-1] // 128
    return np.ascontiguousarray(a.reshape(lead, n, 128).transpose(2, 0, 1).reshape(128, lead * n))


def host_prep(inputs):
    x = np.asarray(inputs["x"], dtype=np.float32)
    vl, nv = vec_layout()
    rl, nr = rep_layout()
    common_vec = np.zeros((128, nv), np.float32)

    def put(name, arr):
        common_vec[:, vl[name]:vl[name] + arr.shape[1]] = arr

    put("n1g", _cols(inputs["norm1_g"]))
    put("n2g", _cols(inputs["norm2_g"]))
    put("psc", _cols(inputs["pool_scale"]))
    put("kvg", _cols(inputs["kv_norm_g"]))
    put("fing", _cols(inputs["final_g"]))
    put("bq", _cols(inputs["b_q"]))
    put("bo", _cols(inputs["b_o"]))
    put("cw", _cols(np.asarray(inputs["ffn_conv_w"]).reshape(12, 5632)))
    put("cb", _cols(inputs["ffn_conv_b"]))
    bkv = np.asarray(inputs["b_kv"], np.float32)
    z64 = np.zeros(64, np.float32)
    bk = np.stack([np.concatenate([bkv[0:64], z64]), np.concatenate([z64, bkv[0:64]]),
                   np.concatenate([bkv[64:128], z64]), np.concatenate([z64, bkv[64:128]])], axis=1)
    put("bk", bk)

    s_idx = np.arange(128)[:, None]
    q_idx = np.arange(128)[None, :]
    m_prev = (s_idx > q_idx).astype(np.float32)
    m_diag = (s_idx <= q_idx).astype(np.float32)
    ident = np.eye(128, dtype=np.float32)

    weights = {k: np.ascontiguousarray(np.asarray(inputs[k], dtype=np.float32))
               for k in ["pool_w", "w_kv", "w_q", "w_o", "ffn_up", "ffn_down"]}
    wkv = weights["w_kv"]
    wk_pad = np.zeros((D, 2, 2, 128), np.float32)
    for g in range(2):
        wk_pad[:, g, 0, 0:64] = wkv[:, g * 64:(g + 1) * 64]
        wk_pad[:, g, 1, 64:128] = wkv[:, g * 64:(g + 1) * 64]
    weights["wk_pad"] = wk_pad.reshape(D, 512)
    in_maps = []
    for core in range(NCORES):
        b = core // 4
        t0 = (core % 4) * TOWN
        first = (core % 4) == 0
        xt = np.zeros((D, T), np.float32)
        if first:
            xt[:, HALO:] = x[b, 0:TOWN, :].T
        else:
            xt[:, :] = x[b, t0 - HALO:t0 + TOWN, :].T
        vec = common_vec.copy()
        vec[:, vl["flag"]] = 0.0 if first else 1.0
        rep = np.zeros((128, nr), np.float32)
        rep[:, rl["bv"]:rl["bv"] + 128] = bkv[128:256][None, :]
        rep[:, rl["sink"]:rl["sink"] + 32] = np.asarray(inputs["sinks"], np.float32).reshape(1, 32)
        inv = np.zeros((8, 16), np.float32)
        for c in range(8):
            win = 2 ** (c // 2 + 1)
            for tt in range(16):
                cnt = min(tt + 1, win) if first else win
                inv[c, tt] = 1.0 / cnt
        rep[:, rl["invcnt"]:rl["invcnt"] + 128] = inv.reshape(1, 128)
        masks = np.stack([m_prev, m_diag, m_prev * (0.0 if first else 1.0), m_diag], axis=1)
        m = {"xT": xt, "vecs": vec, "reps": rep, "masks": np.ascontiguousarray(masks), "ident": ident}
        m.update(weights)
        in_maps.append(m)
    return in_maps


POOL_START = 64
ATTN_FFN_START = 248


def macro_tiles(kind):
    res = []
    for m in range(T // MACRO):
        t0 = m * MACRO
        subs = [(t0, SUB), (t0 + SUB, SUB)]
        if m == 0:
            first = {"pool": POOL_START, "attn": 128, "attn_ffn": ATTN_FFN_START, "std": 0}[kind]
            subs = [(first, SUB - first), (SUB, SUB)]
        res.append((t0, subs))
    return res


def build_program(stop_after=None):
    nc = bass.Bass("TRN2", target_bir_lowering=False)
    vl, nv = vec_layout()
    rl, nr = rep_layout()
    dr = {}
    dr["xT"] = nc.dram_tensor("xT", [D, T], F32, kind="ExternalInput").ap()
    dr["vecs"] = nc.dram_tensor("vecs", [128, nv], F32, kind="ExternalInput").ap()
    dr["reps"] = nc.dram_tensor("reps", [128, nr], F32, kind="ExternalInput").ap()
    dr["masks"] = nc.dram_tensor("masks", [128, 4, 128], F32, kind="ExternalInput").ap()
    dr["ident"] = nc.dram_tensor("ident", [128, 128], F32, kind="ExternalInput").ap()
    dr["pool_w"] = nc.dram_tensor("pool_w", [2, 4, 256, 256], F32, kind="ExternalInput").ap()
    dr["w_kv"] = nc.dram_tensor("w_kv", [1024, 256], F32, kind="ExternalInput").ap()
    dr["wk_pad"] = nc.dram_tensor("wk_pad", [1024, 512], F32, kind="ExternalInput").ap()
    dr["w_q"] = nc.dram_tensor("w_q", [2, 1024, 1024], F32, kind="ExternalInput").ap()
    dr["w_o"] = nc.dram_tensor("w_o", [2, 1024, 1024], F32, kind="ExternalInput").ap()
    dr["ffn_up"] = nc.dram_tensor("ffn_up", [4, 1024, 5632], F32, kind="ExternalInput").ap()
    dr["ffn_down"] = nc.dram_tensor("ffn_down", [4, 2816, 1024], F32, kind="ExternalInput").ap()
    if stop_after is None:
        yT = nc.dram_tensor("yT", [D, TOWN], F32, kind="ExternalOutput").ap()
    else:
        dbg = nc.dram_tensor("dbg", [D, T], F32, kind="ExternalOutput").ap()

    P = Prog()
    with ExitStack() as st:
        def sb(name, shape, dt):
            return st.enter_context(nc.sbuf_tensor(name, shape, dt))

        def psb(name, shape, dt):
            return st.enter_context(nc.psum_tensor(name, shape, dt))

        x_sb = sb("x_sb", [128, NCH, T], F32)
        hn = sb("hn", [128, NCH, HOFF + MACRO], BF16)
        G = sb("G", [128, 12, MACRO], BF16)
        pooled = sb("pooled", [128, NCH, MACRO], BF16)
        hn1 = sb("hn1", [128, NCH, HOFF + MACRO], BF16)
        _pf = pooled[:].rearrange("p c t -> p (c t)")
        Pt = [_pf[:, i * 512:(i + 1) * 512] for i in range(4)]
        Osb = [_pf[:, 2048 + i * 1024:2048 + (i + 1) * 1024].rearrange("p (h d) -> p h d", h=16) for i in range(2)]
        wr = [sb(f"wr{i}", [128, SLOT_ELEMS], BF16) for i in range(NSLOT)]
        KT = sb("KT", [128, 2, 2, T], BF16)
        Vaug = sb("Vaug", [128, NBLK, 2, 65], BF16)
        ftmp = [sb(f"ftmp{i}", [128, SUB], F32) for i in range(6)]
        sq = sb("sq", [128, NCH, SUB], BF16)
        rstd2 = sb("rstd", [128, 2, SUB], F32)
        rtmp = sb("rtmp", [128, SUB], F32)
        epsc = sb("epsc", [128, 1], F32)
        pA = sb("pA", [128, HOFF + MACRO], F32)
        pB = sb("pB", [128, HOFF + MACRO], F32)
        ptmp = sb("ptmp", [128, 16], F32)
        carry1 = sb("carry1", [128, NCH, 16], BF16)
        carry2 = sb("carry2", [128, NCH, 2], BF16)
        NPT = 4
        den = sb("den", [128, 16], F32)
        rden = sb("rden", [128, 16], F32)
        esink = sb("esink", [128, 32], F32)
        vecs = sb("vecs_sb", [128, nv], F32)
        reps = sb("reps_sb", [128, nr], F32)
        masks = sb("masks_sb", [128, 4, 128], BF16)
        ident = sb("ident_sb", [128, 128], BF16)
        ones = sb("ones_sb", [128, 128], BF16)

        pa = [psb(f"pa{i}", [128, 512], F32) for i in range(4)]
        pb = [psb(f"pb{i}", [128, 512], F32) for i in range(2)]
        pn = psb("pn", [128, 512], F32)
        pt = psb("pt", [128, 1024], BF16)

        def vcol(name, i):
            o = vl[name] + i
            return vecs[:, o:o + 1]

        def mm(out, lhsT, rhs, start, stop, reads, writes):
            P.add("pe", lambda e: e.matmul(out, lhsT=lhsT, rhs=rhs, start=start, stop=stop), reads, writes)

        def tr(out, in_, reads, writes):
            P.add("pe", lambda e: e.transpose(out, in_, ident[:]), list(reads) + ["ident"], writes)

        def act(out, in_, func, reads, writes, bias=None, scale=None):
            kw = {}
            if bias is not None:
                kw["bias"] = bias
            if scale is not None:
                kw["scale"] = scale
            P.add("act", lambda e: e.activation(out=out, in_=in_, func=func, **kw), reads, writes)

        def ts(eng, out, in0, s1, s2, op0, op1, reads, writes):
            if op1 is None:
                P.add(eng, lambda e: e.tensor_scalar(out, in0, s1, None, op0), reads, writes)
            else:
                P.add(eng, lambda e: e.tensor_scalar(out, in0, s1, s2, op0, op1), reads, writes)

        def stt(eng, out, in0, scalar, in1, op0, op1, reads, writes):
            P.add(eng, lambda e: e.scalar_tensor_tensor(out, in0, scalar, in1, op0, op1), reads, writes)

        def tt(eng, out, in0, in1, op, reads, writes):
            P.add(eng, lambda e: e.tensor_tensor(out, in0, in1, op), reads, writes)

        def cp(eng, out, in_, reads, writes):
            P.add(eng, lambda e: e.tensor_copy(out, in_), reads, writes)

        def mset(eng, ap, val, writes):
            P.add(eng, lambda e: e.memset(ap, val), (), writes)

        def dma(q, out, in_, semkey, reads, writes, group=None):
            P.add(q, lambda e: e.dma_start(out=out, in_=in_), reads, writes, dma=semkey, group=group)

        def xkeys(cs, t0, ln):
            return [("x", c, b) for c in cs for b in range(t0 // 128, (t0 + ln + 127) // 128)]

        ALLC = list(range(NCH))

        ring = {"plan": [], "issued": 0, "cur": 0}

        def slot_keys(s):
            return [("ws", s, i) for i in range(MAXPARTS)]

        def issue_upto(k):
            while ring["issued"] < min(k, len(ring["plan"])):
                i = ring["issued"]
                s = i % NSLOT
                tag, parts = ring["plan"][i]
                for pi, (ovf, iv) in enumerate(parts):
                    wk = [("ws", s, pi)]
                    if pi == len(parts) - 1:
                        wk = [("ws", s, j) for j in range(pi, MAXPARTS)]
                    dma("pool", ovf(wr[s]), iv, f"ws{s}", (), wk, group=i)
                ring["issued"] += 1

        def acquire(tag):
            k = ring["cur"]
            assert ring["plan"][k][0] == tag, (ring["plan"][k][0], tag)
            issue_upto(k + NSLOT)
            ring["cur"] += 1
            s = k % NSLOT
            return wr[s], slot_keys(s)

        def spec_pool_w(l):
            return (("poolw", l), [(lambda w: w[:, 0:2048].rearrange("p (g k d) -> p g k d", g=4, k=2),
                                    dr["pool_w"][l].rearrange("g (k p) d -> p g k d", p=128))])

        def spec_up(l, pp):
            upv = dr["ffn_up"][l].rearrange("(k p) f -> p k f", p=128)
            parts = []
            for gv in range(2):
                c0 = gv * DFF + pp * 256
                parts.append(((lambda gv: lambda w: w[:, 0:4096].rearrange("p (k g c) -> p k g c", k=8, g=2)[:, :, gv, :])(gv),
                              upv[:, :, c0:c0 + 256]))
            return (("up", l, pp), parts)

        def spec_down(l, half, dq):
            f0 = 0 if half == 0 else HALF_PAIRS[0]
            nf = HALF_PAIRS[half]
            dv = dr["ffn_down"][l].rearrange("(f p) d -> p f d", p=128)
            return (("down", l, half, dq), [((lambda nf: lambda w: w[:, 0:nf * 256].rearrange("p (f d) -> p f d", f=nf))(nf),
                                             dv[:, f0:f0 + nf, dq * 256:(dq + 1) * 256])])

        def spec_wq(j, hf):
            v = dr["w_q"][j].rearrange("(k p) f -> p k f", p=128)
            return (("wq", j, hf), [(lambda w: w[:, 0:4096].rearrange("p (k c) -> p k c", k=8), v[:, :, hf * 512:(hf + 1) * 512])])

        def spec_wo(j, hf):
            v = dr["w_o"][j].rearrange("(k p) f -> p k f", p=128)
            return (("wo", j, hf), [(lambda w: w[:, 0:4096].rearrange("p (k c) -> p k c", k=8), v[:, :, hf * 512:(hf + 1) * 512])])

        def spec_wkv():
            v = dr["w_kv"].rearrange("(k p) f -> p k f", p=128)
            vk = dr["wk_pad"].rearrange("(k p) f -> p k f", p=128)
            return [(("wkvK",), [(lambda w: w[:, 0:4096].rearrange("p (k c) -> p k c", k=8), vk)]),
                    (("wkvV",), [(lambda w: w[:, 0:1024].rearrange("p (k c) -> p k c", k=8), v[:, :, 128:256])])]

        def stop_here(l, stage):
            return stop_after is not None and tuple(stop_after) == (l, stage)

        done = False
        for l in range(DEPTH):
            attn = l >= N_A
            for (t0, subs) in macro_tiles("attn" if attn else "pool"):
                if attn:
                    for _ in subs:
                        ring["plan"] += [spec_wq(l - N_A, 0), spec_wq(l - N_A, 1), spec_wo(l - N_A, 0), spec_wo(l - N_A, 1)]
                else:
                    ring["plan"].append(spec_pool_w(l))
                if not (stop_after is not None and tuple(stop_after) == (l, "M")):
                    pj = 0
                    for half in range(2):
                        for _ in range(HALF_PAIRS[half] // 2):
                            ring["plan"].append(spec_up(l, pj))
                            pj += 1
                        for dq in range(4):
                            ring["plan"].append(spec_down(l, half, dq))
                    if l == N_A - 1:
                        ring["plan"] += spec_wkv()
            if stop_after is not None and stop_after[0] == l:
                break

        xv = dr["xT"].rearrange("(c p) t -> p c t", p=128)
        for m in range(T // MACRO):
            for c in range(NCH):
                dma("sp", x_sb[:, c, m * MACRO:(m + 1) * MACRO], xv[:, c, m * MACRO:(m + 1) * MACRO], f"xl{m}",
                    (), xkeys([c], m * MACRO, MACRO), group=m)
            if m == 0:
                dma("sp", vecs[:], dr["vecs"], "cvec", (), ["vecs"])
                dma("sp", reps[:], dr["reps"], "crep", (), ["reps"])
        dma("pool", masks[:], dr["masks"], "cmask", (), ["masks"])
        dma("pool", ident[:], dr["ident"], "cident", (), ["ident"])
        issue_upto(NSLOT)
        mset("dve", ones[:], 1.0 / 1024.0, ["ones"])
        mset("pool", hn[:], 0.0, [("hn", c, s_) for c in ALLC for s_ in range(2)] + [("hnh", c) for c in ALLC])
        mset("pool", hn1[:], 0.0, [("h1", c, s_) for c in ALLC for s_ in range(2)] + [("h1h", c) for c in ALLC])
        mset("dve", epsc[:], EPS, ["epsc"])
        mset("dve", Vaug[:, :, :, 64:65], 1.0, ["vones"])
        act(esink[:], reps[:, rl["sink"]:rl["sink"] + 32], AF.Exp, ["reps"], ["esink"])

        def rsqrt_pn(sl, ri=0):
            rstd = rstd2[:, ri, :]
            rk = ("rstd", ri)
            act(rtmp[:, 0:sl], pn[:, 0:sl], AF.Ln, ["pn", "epsc"], ["rtmp"], bias=epsc[:, 0:1])
            act(rstd[:, 0:sl], rtmp[:, 0:sl], AF.Exp, ["rtmp"], [rk], scale=-0.5)

        def norm_stt(c, s0, sl, col0, si, dst, dkey, gname, gidx0):
            return lambda: stt("dve", dst[:, c, col0:col0 + sl], x_sb[:, c, s0:s0 + sl], vcol(gname, gidx0 + c), rstd2[:, si, 0:sl],
                               ALU.mult, ALU.mult, xkeys([c], s0, sl) + [("rstd", si), "vecs"], [(dkey, c, si)])

        def norm(s0, sl, t0, gname, gidx0, defer=False, dst=None, dkey="hn", pad=0, stt_later=False):
            dst = hn if dst is None else dst
            si = 0 if s0 < t0 + SUB else 1
            col0 = HOFF + (s0 - t0)
            todo = []
            D_ = todo.append
            D_(lambda: act(sq[:, 0:4, 0:sl], x_sb[:, 0:4, s0:s0 + sl], AF.Square, xkeys(range(0, 4), s0, sl), [("sq", 0)]))
            D_(lambda: act(sq[:, 4:8, 0:sl], x_sb[:, 4:8, s0:s0 + sl], AF.Square, xkeys(range(4, 8), s0, sl), [("sq", 1)]))
            for _ in range(pad):
                D_(lambda: None)

            def mms():
                for c in range(NCH):
                    mm(pn[:, 0:sl], ones[:], sq[:, c, 0:sl], c == 0, c == NCH - 1, [("sq", c // 4), "ones"], ["pn"])
            D_(mms)
            D_(lambda: rsqrt_pn(sl, si))
            if stt_later:
                return todo
            for c in range(NCH):
                D_(norm_stt(c, s0, sl, col0, si, dst, dkey, gname, gidx0))
            if defer:
                return todo
            for f in todo:
                f()
            return si

        def zero_halo(t0, subs):
            if t0 != 0:
                return
            lo = subs[0][0]
            ts("dve", x_sb[:, :, lo:HALO], x_sb[:, :, lo:HALO], vcol("flag", 0), None, ALU.mult, None,
               xkeys(ALLC, lo, HALO - lo) + ["vecs"], xkeys(ALLC, lo, HALO - lo))

        def pool_pre(l, t0, subs):
            L = MACRO
            W = HOFF + L
            todo = []
            D_ = todo.append
            if t0 == 0:
                D_(lambda: mset("pool", hn1[:, :, 0:HOFF], 0.0, [("h1h", c) for c in ALLC]))
            else:
                D_(lambda: cp("pool", hn1[:, :, 0:HOFF], carry1[:], ["carry1"], [("h1h", c) for c in ALLC]))
            for (s0, sl) in subs:
                todo.extend(norm(s0, sl, t0, "n1g", l * 8, defer=True, dst=hn1, dkey="h1", pad=PAD, stt_later=True))
            for c in range(NCH):
                nl = c // 2 + 1
                win = 2 ** nl
                hk = [("h1", c, 0), ("h1", c, 1), ("h1h", c)]
                for (s0, sl) in subs:
                    si_ = 0 if s0 < t0 + SUB else 1
                    D_(norm_stt(c, s0, sl, HOFF + (s0 - t0), si_, hn1, "h1", "n1g", l * 8))
                src = hn1[:, c, :]
                bufs = [pA, pB]
                cur = None
                for lev in range(nl):
                    sh = 2 ** lev
                    lo = 2 * sh - 1
                    dst = bufs[lev % 2]
                    dk = "pA" if lev % 2 == 0 else "pB"
                    if lev == 0:
                        D_((lambda dst, lo, sh, hk, dk, src: lambda: tt("dve", dst[:, lo:W], src[:, lo:W], src[:, lo - sh:W - sh], ALU.add, hk, [dk]))(dst, lo, sh, hk, dk, src))
                    else:
                        ck = "pA" if (lev - 1) % 2 == 0 else "pB"
                        D_((lambda dst, lo, sh, ck, dk, cur: lambda: tt("dve", dst[:, lo:W], cur[:, lo:W], cur[:, lo - sh:W - sh], ALU.add, [ck], [dk]))(dst, lo, sh, ck, dk, cur))
                    cur = dst
                ck = "pA" if (nl - 1) % 2 == 0 else "pB"
                D_((lambda c, cur, win, src, ck, hk: lambda: stt("dve", pooled[:, c, 0:L], cur[:, HOFF:W], 1.0 / win, src[:, HOFF:W], ALU.mult, ALU.subtract,
                                                                 [ck] + hk, [("pl", c, 0), ("pl", c, 1)]))(c, cur, win, src, ck, hk))
                if t0 == 0:
                    io = rl["invcnt"] + c * 16
                    D_((lambda c, cur, io, ck: lambda: tt("dve", ptmp[:], cur[:, HOFF + HALO:HOFF + HALO + 16], reps[:, io:io + 16], ALU.mult,
                                                          [ck, "reps"], ["ptmp"]))(c, cur, io, ck))
                    D_((lambda c, src, hk: lambda: tt("dve", pooled[:, c, HALO:HALO + 16], ptmp[:], src[:, HOFF + HALO:HOFF + HALO + 16], ALU.subtract,
                                                      ["ptmp"] + hk, [("pl", c, 0)]))(c, src, hk))
            D_(lambda: cp("pool", carry1[:], hn1[:, :, HOFF + L - 16:HOFF + L], [("h1", c, 1) for c in ALLC], ["carry1"]))
            return todo

        def pool_post(l, t0, subs):
            w, wk = acquire(("poolw", l))
            pw = w[:, 0:2048].rearrange("p (g k d) -> p g k d", g=4, k=2)
            for (s0, sl) in subs:
                si = 0 if s0 < t0 + SUB else 1
                c0 = s0 - t0
                for dc in range(NCH):
                    g = dc // 2
                    ps = pb[dc % 2]
                    for kc in range(2):
                        mm(ps[:, 0:sl], pw[:, g, kc, (dc % 2) * 128:(dc % 2) * 128 + 128], pooled[:, 2 * g + kc, c0:c0 + sl],
                           kc == 0, kc == 1, wk + [("pl", 2 * g + kc, si)], [("pb", dc % 2)])
                    stt("dve", x_sb[:, dc, s0:s0 + sl], ps[:, 0:sl], vcol("psc", l * 8 + dc), x_sb[:, dc, s0:s0 + sl],
                        ALU.mult, ALU.add, [("pb", dc % 2), "vecs"] + xkeys([dc], s0, sl), xkeys([dc], s0, sl))
            zero_halo(t0, subs)

        def ffn_stage(l, t0, subs, first_tile, inject=None):
            L = MACRO
            if first_tile:
                mset("pool", hn[:, :, HOFF - 2:HOFF], 0.0, [("hnh", c) for c in ALLC])
            else:
                cp("pool", hn[:, :, HOFF - 2:HOFF], carry2[:], ["carry2"], [("hnh", c) for c in ALLC])
            for (s0, sl) in subs:
                norm(s0, sl, t0, "n2g", l * 8)
            cp("pool", carry2[:], hn[:, :, HOFF + L - 2:HOFF + L], [("hn", c, 1) for c in ALLC], ["carry2"])
            hsub = [[[("hn", c, 0), ("hnh", c)] for c in ALLC],
                    [[("hn", c, 1), ("hn", c, 0)] for c in ALLC]]
            cwb = l * 3 * 44
            pj_base = 0
            fbuf = 0
            for half in range(2):
                npair = HALF_PAIRS[half]
                for pp in range(npair // 2):
                    w, wk = acquire(("up", l, pj_base // 2 + pp))
                    upw = w[:, 0:4096].rearrange("p (k g c) -> p k g c", k=8, g=2)
                    for (s0, sl) in subs:
                        for pj in range(2):
                            j = pj_base + pp * 2 + pj
                            jj = pp * 2 + pj
                            si = 0 if s0 < t0 + SUB else 1
                            c0 = HOFF + (s0 - t0)
                            pg = pa[2 * fbuf]
                            pv = pa[2 * fbuf + 1]
                            Ag, Av, Sg = ftmp[3 * fbuf], ftmp[3 * fbuf + 1], ftmp[3 * fbuf + 2]
                            kg, kv_, kA, kV, kS = ("pa", 2 * fbuf), ("pa", 2 * fbuf + 1), ("ft", 3 * fbuf), ("ft", 3 * fbuf + 1), ("ft", 3 * fbuf + 2)
                            for gv, (ps, pk) in enumerate([(pg, kg), (pv, kv_)]):
                                for k in range(NCH):
                                    mm(ps[:, 0:sl + 2], upw[:, k, gv, pj * 128:(pj + 1) * 128], hn[:, k, c0 - 2:c0 + sl],
                                       k == 0, k == NCH - 1, wk + hsub[si][k], [pk])
                            for gv, (ps, pk, A, ak) in enumerate([(pg, kg, Ag, kA), (pv, kv_, Av, kV)]):
                                fc = gv * NFC + j
                                w0 = vcol("cw", cwb + 0 * 44 + fc)
                                w1 = vcol("cw", cwb + 1 * 44 + fc)
                                w2 = vcol("cw", cwb + 2 * 44 + fc)
                                bb = vcol("cb", l * 44 + fc)
                                act(A[:, 0:sl], ps[:, 2:sl + 2], AF.Identity, [pk, "vecs"], [ak], bias=bb, scale=w2)
                                stt("dve", A[:, 0:sl], ps[:, 1:sl + 1], w1, A[:, 0:sl], ALU.mult, ALU.add, [pk, ak, "vecs"], [ak])
                                stt("dve", A[:, 0:sl], ps[:, 0:sl], w0, A[:, 0:sl], ALU.mult, ALU.add, [pk, ak, "vecs"], [ak])
                            act(Sg[:, 0:sl], Ag[:, 0:sl], AF.Silu, [kA], [kS])
                            tt("pool", G[:, jj, s0 - t0:s0 - t0 + sl], Sg[:, 0:sl], Av[:, 0:sl], ALU.mult, [kS, kV], [("G", jj, si)])
                            fbuf ^= 1
                bbanks = [(pb[0], ("pb", 0)), (pb[1], ("pb", 1)), (pa[2], ("pa", 2)), (pa[3], ("pa", 3))]
                bctr = 0
                if half == 0:
                    inj = list(inject) if inject is not None else []
                    ngroups = 16 * len(subs) - 4
                    per = (len(inj) + ngroups - 1) // ngroups if inj else 0
                for dq in range(4):
                    w, wk = acquire(("down", l, half, dq))
                    dw = w[:, 0:npair * 256].rearrange("p (f d) -> p f d", f=npair)
                    for dl in range(2):
                        dc = dq * 2 + dl
                        for (s0, sl) in subs:
                            si = 0 if s0 < t0 + SUB else 1
                            ps, pk = bbanks[bctr % 4]
                            bctr += 1
                            for f in range(npair):
                                mm(ps[:, 0:sl], dw[:, f, dl * 128:(dl + 1) * 128], G[:, f, s0 - t0:s0 - t0 + sl],
                                   f == 0, f == npair - 1, wk + [("G", f, si)], [pk])
                            tt("dve", x_sb[:, dc, s0:s0 + sl], ps[:, 0:sl], x_sb[:, dc, s0:s0 + sl], ALU.add,
                               [pk] + xkeys([dc], s0, sl), xkeys([dc], s0, sl))
                            for _ in range(per):
                                if inj:
                                    inj.pop(0)()
                if half == 1:
                    while inj:
                        inj.pop(0)()
                pj_base += npair
            zero_halo(t0, subs)

        def kv_stage(t0, subs):
            for (s0, sl) in subs:
                norm(s0, sl, t0, "kvg", 0)
            w, wk = acquire(("wkvK",))
            kw = w[:, 0:4096].rearrange("p (k g h c) -> p k g h c", k=8, g=2, h=2)
            for (s0, sl) in subs:
                si = 0 if s0 < t0 + SUB else 1
                c0 = HOFF + (s0 - t0)
                hk = [("hn", c, si) for c in ALLC]
                for g in range(2):
                    for hh in range(2):
                        for k in range(NCH):
                            mm(pb[hh][:, 0:sl], kw[:, k, g, hh, :], hn[:, k, c0:c0 + sl], k == 0, k == NCH - 1, wk + hk, [("pb", hh)])
                        act(KT[:, hh, g, s0:s0 + sl], pb[hh][:, 0:sl], AF.Identity, [("pb", hh), "vecs"],
                            [("KT", s0 // 128 + i) for i in range(sl // 128)], bias=vcol("bk", g * 2 + hh))
            w2, wk2 = acquire(("wkvV",))
            vw = w2[:, 0:1024].rearrange("p (k c) -> p k c", k=8)
            for (s0, sl) in subs:
                si = 0 if s0 < t0 + SUB else 1
                c0 = HOFF + (s0 - t0)
                hk = [("hn", c, si) for c in ALLC]
                for bi in range(sl // 128):
                    blk = s0 // 128 + bi
                    for k in range(NCH):
                        mm(pn[:, 0:128], hn[:, k, c0 + bi * 128:c0 + (bi + 1) * 128], vw[:, k, :], k == 0, k == NCH - 1, wk2 + hk, ["pn"])
                    tt("dve", Vaug[:, blk, :, 0:64], pn[:, 0:128].rearrange("p (g d) -> p g d", g=2),
                       reps[:, rl["bv"]:rl["bv"] + 128].rearrange("p (g d) -> p g d", g=2), ALU.add, ["pn", "reps", "vones"], [("V", blk)])

        OAK = [("pa", 2), ("pa", 3), "pn"]

        def oaug_ap(h):
            bank = [pa[2], pa[3], pn][h // 7]
            o = (h % 7) * 65
            return bank[:, o:o + 65], OAK[h // 7]

        def attn_pre(l, t0, subs):
            todo = []
            for (s0, sl) in subs:
                todo.extend(norm(s0, sl, t0, "n1g", l * 8, defer=True, dst=hn1, dkey="h1", pad=PAD))
            return todo

        def attn_stage(l, t0, subs):
            j = l - N_A
            QT = lambda c, a, b: G[:, c, a:b]
            OT = lambda c, a, b: G[:, c, SUB + a:SUB + b]
            for (s0, sl) in subs:
                si = 0 if s0 < t0 + SUB else 1
                c0 = HOFF + (s0 - t0)
                hk = [("h1", c, si) for c in ALLC]
                for hf in range(2):
                    w, wk = acquire(("wq", j, hf))
                    wv_ = w[:, 0:4096].rearrange("p (k c) -> p k c", k=8)
                    for cl in range(4):
                        c = hf * 4 + cl
                        ps, pk = pb[c % 2], ("pb", c % 2)
                        for k in range(NCH):
                            mm(ps[:, 0:sl], wv_[:, k, cl * 128:(cl + 1) * 128], hn1[:, k, c0:c0 + sl], k == 0, k == NCH - 1, wk + hk, [pk])
                        act(QT(c, 0, sl), ps[:, 0:sl], AF.Identity, [pk, "vecs"], [("G", c, 0)], bias=vcol("bq", j * 8 + c))
                nqb = sl // 128
                stbanks = [(pa[0], ("pa", 0)), (pa[1], ("pa", 1)), (pb[0], ("pb", 0)), (pb[1], ("pb", 1))]

                def qk(qb, hp):
                    n = s0 // 128 + qb
                    mview = masks[:, 2:4, :] if n == 2 else masks[:, 0:2, :]
                    g = hp // 4
                    stp, sk = stbanks[hp % 4]
                    for hh in range(2):
                        for kb in range(2):
                            kblk = n - 1 + kb
                            mm(stp[:, (hh * 2 + kb) * 128:(hh * 2 + kb + 1) * 128],
                               KT[:, hh, g, kblk * 128:(kblk + 1) * 128],
                               QT(hp, qb * 128, (qb + 1) * 128), True, True,
                               [("KT", kblk), ("G", hp, 0)], [sk])
                    pbuf = hp % NPT
                    Pk = ("Pt", pbuf)
                    act(Pt[pbuf], stp[:], AF.Exp, [sk], [Pk], scale=0.125)
                    pv4 = Pt[pbuf].rearrange("p (h k q) -> p h k q", h=2, k=2)
                    tt("dve", pv4, pv4, mview.unsqueeze(1).to_broadcast([128, 2, 2, 128]), ALU.mult, [Pk, "masks"], [Pk])

                def pvm(qb, hp):
                    n = s0 // 128 + qb
                    g = hp // 4
                    pbuf = hp % NPT
                    Pk = ("Pt", pbuf)
                    for hh in range(2):
                        h = hp * 2 + hh
                        oa, oak = oaug_ap(h)
                        for kb in range(2):
                            kblk = n - 1 + kb
                            mm(oa, Pt[pbuf][:, (hh * 2 + kb) * 128:(hh * 2 + kb + 1) * 128], Vaug[:, kblk, g, :],
                               kb == 0, kb == 1, [Pk, ("V", kblk), "vones"], [oak])

                def evac(qb, bi):
                    obuf = qb % 2
                    O = Osb[obuf]
                    h0, nh = [(0, 7), (7, 7), (14, 2)][bi]
                    bank = [pa[2], pa[3], pn][bi]
                    bv = bank[:, 0:nh * 65].rearrange("p (h d) -> p h d", h=nh)
                    dk, rk = ("den", bi), ("rden", bi)
                    tt("dve", den[:, h0:h0 + nh], bv[:, :, 64], esink[:, j * 16 + h0:j * 16 + h0 + nh], ALU.add,
                       [OAK[bi], "esink"], [dk])
                    P.add("dve", lambda e: e.reciprocal(rden[:, h0:h0 + nh], den[:, h0:h0 + nh]), [dk], [rk])
                    tt("dve", O[:, h0:h0 + nh, :], bv[:, :, 0:64], rden[:, h0:h0 + nh].unsqueeze(2).to_broadcast([128, nh, 64]),
                       ALU.mult, [OAK[bi], rk], [("Osb", obuf, bi)])

                def epilogue_pe(qb):
                    obuf = qb % 2
                    Of = Osb[obuf].rearrange("p h d -> p (h d)")
                    oks = [("Osb", obuf, bi) for bi in range(3)]
                    for c in range(NCH):
                        tr(pt[:, c * 128:(c + 1) * 128], Of[:, c * 128:(c + 1) * 128], oks, ["pt"])
                    act(G[:, 0:8, SUB + qb * 128:SUB + (qb + 1) * 128], pt[:].rearrange("p (c q) -> p c q", c=8), AF.Copy,
                        ["pt"], [("G", c, 1) for c in ALLC])

                units = [(qb, hp) for qb in range(nqb) for hp in range(8)]
                LA = 3
                for u in range(LA):
                    qk(*units[u])
                for u, (qb, hp) in enumerate(units):
                    if u + LA < len(units):
                        qk(*units[u + LA])
                    pvm(qb, hp)
                    if hp == 3:
                        evac(qb, 0)
                    if hp == 6:
                        evac(qb, 1)
                    if hp == 7:
                        evac(qb, 2)
                    if hp == 4 and qb > 0:
                        epilogue_pe(qb - 1)
                epilogue_pe(nqb - 1)
                o0 = max(s0, ATTN_FFN_START) - s0
                ol = sl - o0
                for hf in range(2):
                    w, wk = acquire(("wo", j, hf))
                    wv_ = w[:, 0:4096].rearrange("p (k c) -> p k c", k=8)
                    for dl in range(4):
                        dc = hf * 4 + dl
                        ps, pk = pb[dc % 2], ("pb", dc % 2)
                        for c in range(NCH):
                            mm(ps[:, 0:ol], wv_[:, c, dl * 128:(dl + 1) * 128], OT(c, o0, sl), c == 0, c == NCH - 1,
                               wk + [("G", c, 1)], [pk])
                        stt("dve", x_sb[:, dc, s0 + o0:s0 + sl], ps[:, 0:ol], vcol("bo", j * 8 + dc), x_sb[:, dc, s0 + o0:s0 + sl],
                            ALU.add, ALU.add, [pk, "vecs"] + xkeys([dc], s0, sl), xkeys([dc], s0, sl))
            zero_halo(t0, subs)

        def dump_x():
            dv = dbg.rearrange("(c p) t -> p c t", p=128)
            for c in range(NCH):
                dma("sp", dv[:, c, :], x_sb[:, c, :], "out", xkeys([c], 0, T), [("y", c)])
            P.add("sp", lambda e: e.nop(), [("y", c) for c in ALLC], ())

        stopped = False
        pre_done = set()
        fin = {"nout": 0, "oi": 0}

        def final_norm(t0):
            yv = yT.rearrange("(c p) t -> p c t", p=128)
            obufs = [(pA[:, 0:SUB], "pA"), (pA[:, SUB:2 * SUB], "pA2"), (pB[:, 0:SUB], "pB"), (pB[:, SUB:2 * SUB], "pB2")]
            for (s0, sl) in [(t0, SUB), (t0 + SUB, SUB)]:
                if s0 + sl <= HALO:
                    continue
                if s0 < HALO:
                    s0, sl = HALO, s0 + sl - HALO
                act(sq[:, 0:4, 0:sl], x_sb[:, 0:4, s0:s0 + sl], AF.Square, xkeys(range(0, 4), s0, sl), [("sq", 0)])
                act(sq[:, 4:8, 0:sl], x_sb[:, 4:8, s0:s0 + sl], AF.Square, xkeys(range(4, 8), s0, sl), [("sq", 1)])
                for c in range(NCH):
                    mm(pn[:, 0:sl], ones[:], sq[:, c, 0:sl], c == 0, c == NCH - 1, [("sq", c // 4), "ones"], ["pn"])
                rsqrt_pn(sl, 0)
                for c in range(NCH):
                    ob, okk = obufs[fin["oi"] % 4]
                    semk = f"out{fin['oi'] % 4}"
                    fin["oi"] += 1
                    stt("dve", ob[:, 0:sl], x_sb[:, c, s0:s0 + sl], vcol("fing", c), rstd2[:, 0, 0:sl], ALU.mult, ALU.mult,
                        xkeys([c], s0, sl) + [("rstd", 0), "vecs"], [okk])
                    dma("sp", yv[:, c, s0 - HALO:s0 - HALO + sl], ob[:, 0:sl], semk, [okk], [("y", fin["nout"])])
                    fin["nout"] += 1

        def pre_of(l, mi):
            attn = l >= N_A
            t0, subs = macro_tiles("attn" if attn else "pool")[mi]
            return attn_pre(l, t0, subs) if attn else pool_pre(l, t0, subs)

        for l in range(DEPTH):
            attn = l >= N_A
            tiles = macro_tiles("attn" if attn else "pool")
            for mi, (t0, subs) in enumerate(tiles):
                fsubs = macro_tiles("attn_ffn")[mi][1] if attn else subs
                if (l, mi) not in pre_done:
                    for f in pre_of(l, mi):
                        f()
                if attn:
                    attn_stage(l, t0, subs)
                else:
                    pool_post(l, t0, subs)
                if stop_here(l, "M"):
                    continue
                inject = None
                nxt = (l, mi + 1) if mi + 1 < len(tiles) else (l + 1, 0)
                if nxt[0] < DEPTH and not (stop_after is not None and stop_after[0] == l and nxt[0] != l):
                    inject = pre_of(*nxt)
                    pre_done.add(nxt)
                ffn_stage(l, t0, fsubs, mi == 0, inject)
                if l == N_A - 1:
                    kv_stage(t0, macro_tiles("std")[mi][1])
                if l == DEPTH - 1 and stop_after is None:
                    final_norm(t0)
            if stop_after is not None and stop_after[0] == l:
                dump_x()
                stopped = True
                break

        if not stopped:
            P.add("sp", lambda e: e.nop(), [("y", i) for i in range(fin["nout"])], ())

        assert ring["cur"] == len(ring["plan"]), (ring["cur"], len(ring["plan"]))
        sems = {e: st.enter_context(nc.semaphore("s_" + e)) for e in ENG_NAMES}
        dma_sems = {k: st.enter_context(nc.semaphore("d_" + k)) for k in P.dma_keys()}
        block = st.enter_context(nc.Block())
        P.emit(sems, dma_sems, block)
    return nc


_NC_CACHE = {}


def kernel(**inputs):
    in_maps = host_prep(inputs)
    if "full" not in _NC_CACHE:
        _NC_CACHE["full"] = build_program(None)
    nc = _NC_CACHE["full"]
    res = run_bass_kernel_spmd(nc, in_maps, core_ids=list(range(NCORES)))
    out = np.empty((2, SEQ, D), np.float32)
    for core in range(NCORES):
        b = core // 4
        t0 = (core % 4) * TOWN
        out[b, t0:t0 + TOWN, :] = np.asarray(res.results[core]["yT"]).T
    return out
```

```python
import numpy as np
from contextlib import ExitStack
import concourse.bass as bass
import concourse.mybir as mybir
from concourse.bass_utils import run_bass_kernel_spmd

F32 = mybir.dt.float32
BF16 = mybir.dt.bfloat16
AF = mybir.ActivationFunctionType
ALU = mybir.AluOpType

NCORES = 8
D = 1024
NCH = 8
SEQ = 8192
TOWN = 2048
HALO = 256
T = TOWN + HALO
NBLK = T // 128
DFF = 2816
NFC = 22
DEPTH = 4
N_A = 2
EPS = 1e-5
HOFF = 16
MACRO = 768
SUB = 384
NSLOT = 3
PAD = 3
SLOT_ELEMS = 4096
MAXPARTS = 5
HALF_PAIRS = (12, 10)

ENG_NAMES = ["pe", "act", "dve", "pool", "sp"]


class Op:
    __slots__ = ("eng", "fn", "is_dma", "semkey", "waits", "signal", "count", "idx", "group")


class Prog:
    def __init__(self):
        self.ops = []
        self.last_writer = {}
        self.readers = {}

    def add(self, eng, fn, reads=(), writes=(), dma=None, group=None):
        op = Op()
        op.group = group
        op.eng = eng
        op.fn = fn
        op.is_dma = dma is not None
        op.semkey = dma
        op.signal = False
        op.count = None
        op.idx = len(self.ops)
        deps = set()
        for k in reads:
            w = self.last_writer.get(k)
            if w is not None:
                deps.add(w)
        for k in writes:
            w = self.last_writer.get(k)
            if w is not None:
                deps.add(w)
            for r in self.readers.get(k, ()):
                deps.add(r)
        if eng == "pe" and dma is None:
            deps = {d for d in deps if not (self.ops[d].eng == "pe" and not self.ops[d].is_dma)}
        op.waits = deps
        for k in reads:
            self.readers.setdefault(k, []).append(op.idx)
        for k in writes:
            self.last_writer[k] = op.idx
            self.readers[k] = []
        self.ops.append(op)
        return op

    def dma_keys(self):
        return sorted({op.semkey for op in self.ops if op.is_dma})

    def emit(self, sems, dma_sems, block):
        ops = self.ops
        for op in ops:
            for d in op.waits:
                ops[d].signal = True
        cnt = {}
        for op in ops:
            key = ("dma", op.semkey) if op.is_dma else ("eng", op.eng)
            if op.is_dma or op.signal:
                cnt[key] = cnt.get(key, 0) + (16 if op.is_dma else 1)
                op.count = (key, cnt[key])
        gmax = {}
        for op in ops:
            if op.is_dma and op.group is not None:
                gk = (op.semkey, op.group)
                gmax[gk] = max(gmax.get(gk, 0), op.count[1])
        for op in ops:
            if op.is_dma and op.group is not None:
                op.count = (op.count[0], gmax[(op.semkey, op.group)])
        per_eng = {e: [] for e in ENG_NAMES}
        for op in ops:
            per_eng[op.eng].append(op)

        def sem_of(key):
            return dma_sems[key[1]] if key[0] == "dma" else sems[key[1]]

        def run(engname, e):
            seen = {}
            for op in per_eng[engname]:
                need = {}
                for d in op.waits:
                    key, v = ops[d].count
                    if need.get(key, 0) < v:
                        need[key] = v
                for key, v in need.items():
                    if seen.get(key, 0) >= v:
                        continue
                    seen[key] = v
                    e.wait_ge(sem_of(key), v)
                ins = op.fn(e)
                if op.count is not None:
                    ins.then_inc(sem_of(op.count[0]), 16 if op.is_dma else 1)

        @block.tensor
        def _(e):
            run("pe", e)

        @block.scalar
        def _(e):
            run("act", e)

        @block.vector
        def _(e):
            run("dve", e)

        @block.gpsimd
        def _(e):
            run("pool", e)

        @block.sync
        def _(e):
            run("sp", e)


def vec_layout():
    lay = {}
    off = 0
    for name, n in [("n1g", 32), ("n2g", 32), ("psc", 16), ("kvg", 8), ("fing", 8), ("bq", 16),
                    ("bo", 16), ("cw", 4 * 3 * 44), ("cb", 4 * 44), ("bk", 4), ("flag", 1)]:
        lay[name] = off
        off += n
    return lay, off


def rep_layout():
    lay = {}
    off = 0
    for name, n in [("bv", 128), ("sink", 32), ("invcnt", 128)]:
        lay[name] = off
        off += n
    return lay, off


def _cols(a):
    a = np.asarray(a, dtype=np.float32)
    lead = int(np.prod(a.shape[:-1])) if a.ndim > 1 else 1
    n = a.shape[-1] // 128
    return np.ascontiguousarray(a.reshape(lead, n, 128).transpose(2, 0, 1).reshape(128, lead * n))


def host_prep(inputs):
    x = np.asarray(inputs["x"], dtype=np.float32)
    vl, nv = vec_layout()
    rl, nr = rep_layout()
    common_vec = np.zeros((128, nv), np.float32)

    def put(name, arr):
        common_vec[:, vl[name]:vl[name] + arr.shape[1]] = arr

    put("n1g", _cols(inputs["norm1_g"]))
    put("n2g", _cols(inputs["norm2_g"]))
    put("psc", _cols(inputs["pool_scale"]))
    put("kvg", _cols(inputs["kv_norm_g"]))
    put("fing", _cols(inputs["final_g"]))
    put("bq", _cols(inputs["b_q"]))
    put("bo", _cols(inputs["b_o"]))
    put("cw", _cols(np.asarray(inputs["ffn_conv_w"]).reshape(12, 5632)))
    put("cb", _cols(inputs["ffn_conv_b"]))
    bkv = np.asarray(inputs["b_kv"], np.float32)
    z64 = np.zeros(64, np.float32)
    bk = np.stack([np.concatenate([bkv[0:64], z64]), np.concatenate([z64, bkv[0:64]]),
                   np.concatenate([bkv[64:128], z64]), np.concatenate([z64, bkv[64:128]])], axis=1)
    put("bk", bk)

    s_idx = np.arange(128)[:, None]
    q_idx = np.arange(128)[None, :]
    m_prev = (s_idx > q_idx).astype(np.float32)
    m_diag = (s_idx <= q_idx).astype(np.float32)
    ident = np.eye(128, dtype=np.float32)

    weights = {k: np.ascontiguousarray(np.asarray(inputs[k], dtype=np.float32))
               for k in ["pool_w", "w_kv", "w_q", "w_o", "ffn_up", "ffn_down"]}
    wkv = weights["w_kv"]
    wk_pad = np.zeros((D, 2, 2, 128), np.float32)
    for g in range(2):
        wk_pad[:, g, 0, 0:64] = wkv[:, g * 64:(g + 1) * 64]
        wk_pad[:, g, 1, 64:128] = wkv[:, g * 64:(g + 1) * 64]
    weights["wk_pad"] = wk_pad.reshape(D, 512)
    in_maps = []
    for core in range(NCORES):
        b = core // 4
        t0 = (core % 4) * TOWN
        first = (core % 4) == 0
        xt = np.zeros((D, T), np.float32)
        if first:
            xt[:, HALO:] = x[b, 0:TOWN, :].T
        else:
            xt[:, :] = x[b, t0 - HALO:t0 + TOWN, :].T
        vec = common_vec.copy()
        vec[:, vl["flag"]] = 0.0 if first else 1.0
        rep = np.zeros((128, nr), np.float32)
        rep[:, rl["bv"]:rl["bv"] + 128] = bkv[128:256][None, :]
        rep[:, rl["sink"]:rl["sink"] + 32] = np.asarray(inputs["sinks"], np.float32).reshape(1, 32)
        inv = np.zeros((8, 16), np.float32)
        for c in range(8):
            win = 2 ** (c // 2 + 1)
            for tt in range(16):
                cnt = min(tt + 1, win) if first else win
                inv[c, tt] = 1.0 / cnt
        rep[:, rl["invcnt"]:rl["invcnt"] + 128] = inv.reshape(1, 128)
        masks = np.stack([m_prev, m_diag, m_prev * (0.0 if first else 1.0), m_diag], axis=1)
        m = {"xT": xt, "vecs": vec, "reps": rep, "masks": np.ascontiguousarray(masks), "ident": ident}
        m.update(weights)
        in_maps.append(m)
    return in_maps


POOL_START = 64
ATTN_FFN_START = 248


def macro_tiles(kind):
    res = []
    for m in range(T // MACRO):
        t0 = m * MACRO
        subs = [(t0, SUB), (t0 + SUB, SUB)]
        if m == 0:
            first = {"pool": POOL_START, "attn": 128, "attn_ffn": ATTN_FFN_START, "std": 0}[kind]
            subs = [(first, SUB - first), (SUB, SUB)]
            if kind == "attn_ffn":
                h = (MACRO - first) // 2
                subs = [(first, h), (first + h, MACRO - first - h)]
        res.append((t0, subs))
    return res


def build_program(stop_after=None):
    nc = bass.Bass("TRN2", target_bir_lowering=False)
    vl, nv = vec_layout()
    rl, nr = rep_layout()
    dr = {}
    dr["xT"] = nc.dram_tensor("xT", [D, T], F32, kind="ExternalInput").ap()
    dr["vecs"] = nc.dram_tensor("vecs", [128, nv], F32, kind="ExternalInput").ap()
    dr["reps"] = nc.dram_tensor("reps", [128, nr], F32, kind="ExternalInput").ap()
    dr["masks"] = nc.dram_tensor("masks", [128, 4, 128], F32, kind="ExternalInput").ap()
    dr["ident"] = nc.dram_tensor("ident", [128, 128], F32, kind="ExternalInput").ap()
    dr["pool_w"] = nc.dram_tensor("pool_w", [2, 4, 256, 256], F32, kind="ExternalInput").ap()
    dr["w_kv"] = nc.dram_tensor("w_kv", [1024, 256], F32, kind="ExternalInput").ap()
    dr["wk_pad"] = nc.dram_tensor("wk_pad", [1024, 512], F32, kind="ExternalInput").ap()
    dr["w_q"] = nc.dram_tensor("w_q", [2, 1024, 1024], F32, kind="ExternalInput").ap()
    dr["w_o"] = nc.dram_tensor("w_o", [2, 1024, 1024], F32, kind="ExternalInput").ap()
    dr["ffn_up"] = nc.dram_tensor("ffn_up", [4, 1024, 5632], F32, kind="ExternalInput").ap()
    dr["ffn_down"] = nc.dram_tensor("ffn_down", [4, 2816, 1024], F32, kind="ExternalInput").ap()
    if stop_after is None:
        yT = nc.dram_tensor("yT", [D, TOWN], F32, kind="ExternalOutput").ap()
    else:
        dbg = nc.dram_tensor("dbg", [D, T], F32, kind="ExternalOutput").ap()

    P = Prog()
    with ExitStack() as st:
        def sb(name, shape, dt):
            return st.enter_context(nc.sbuf_tensor(name, shape, dt))

        def psb(name, shape, dt):
            return st.enter_context(nc.psum_tensor(name, shape, dt))

        x_sb = sb("x_sb", [128, NCH, T], F32)
        hn = sb("hn", [128, NCH, HOFF + MACRO], BF16)
        G = sb("G", [128, 12, MACRO], BF16)
        pooled = sb("pooled", [128, NCH, MACRO], BF16)
        hn1 = sb("hn1", [128, NCH, HOFF + MACRO], BF16)
        _pf = pooled[:].rearrange("p c t -> p (c t)")
        Pt = [_pf[:, i * 512:(i + 1) * 512] for i in range(4)]
        Osb = [_pf[:, 2048 + i * 1024:2048 + (i + 1) * 1024].rearrange("p (h d) -> p h d", h=16) for i in range(2)]
        wr = [sb(f"wr{i}", [128, SLOT_ELEMS], BF16) for i in range(NSLOT)]
        KT = sb("KT", [128, 2, 2, T], BF16)
        Vaug = sb("Vaug", [128, NBLK, 2, 65], BF16)
        ftmp = [sb(f"ftmp{i}", [128, SUB], F32) for i in range(6)]
        sq = sb("sq", [128, NCH, SUB], BF16)
        rstd2 = sb("rstd", [128, 2, SUB], F32)
        rtmp = sb("rtmp", [128, SUB], F32)
        epsc = sb("epsc", [128, 1], F32)
        pA = sb("pA", [128, HOFF + MACRO], F32)
        pB = sb("pB", [128, HOFF + MACRO], F32)
        ptmp = sb("ptmp", [128, 16], F32)
        carry1 = sb("carry1", [128, NCH, 16], BF16)
        carry2 = sb("carry2", [128, NCH, 2], BF16)
        NPT = 4
        den = sb("den", [128, 16], F32)
        rden = sb("rden", [128, 16], F32)
        esink = sb("esink", [128, 32], F32)
        vecs = sb("vecs_sb", [128, nv], F32)
        reps = sb("reps_sb", [128, nr], F32)
        masks = sb("masks_sb", [128, 4, 128], BF16)
        ident = sb("ident_sb", [128, 128], BF16)
        ones = sb("ones_sb", [128, 128], BF16)

        pa = [psb(f"pa{i}", [128, 512], F32) for i in range(4)]
        pb = [psb(f"pb{i}", [128, 512], F32) for i in range(2)]
        pn = psb("pn", [128, 512], F32)
        pt = psb("pt", [128, 1024], BF16)

        def vcol(name, i):
            o = vl[name] + i
            return vecs[:, o:o + 1]

        def mm(out, lhsT, rhs, start, stop, reads, writes):
            P.add("pe", lambda e: e.matmul(out, lhsT=lhsT, rhs=rhs, start=start, stop=stop), reads, writes)

        def tr(out, in_, reads, writes):
            P.add("pe", lambda e: e.transpose(out, in_, ident[:]), list(reads) + ["ident"], writes)

        def act(out, in_, func, reads, writes, bias=None, scale=None):
            kw = {}
            if bias is not None:
                kw["bias"] = bias
            if scale is not None:
                kw["scale"] = scale
            P.add("act", lambda e: e.activation(out=out, in_=in_, func=func, **kw), reads, writes)

        def ts(eng, out, in0, s1, s2, op0, op1, reads, writes):
            if op1 is None:
                P.add(eng, lambda e: e.tensor_scalar(out, in0, s1, None, op0), reads, writes)
            else:
                P.add(eng, lambda e: e.tensor_scalar(out, in0, s1, s2, op0, op1), reads, writes)

        def stt(eng, out, in0, scalar, in1, op0, op1, reads, writes):
            P.add(eng, lambda e: e.scalar_tensor_tensor(out, in0, scalar, in1, op0, op1), reads, writes)

        def tt(eng, out, in0, in1, op, reads, writes):
            P.add(eng, lambda e: e.tensor_tensor(out, in0, in1, op), reads, writes)

        def cp(eng, out, in_, reads, writes):
            P.add(eng, lambda e: e.tensor_copy(out, in_), reads, writes)

        def mset(eng, ap, val, writes):
            P.add(eng, lambda e: e.memset(ap, val), (), writes)

        def dma(q, out, in_, semkey, reads, writes, group=None):
            P.add(q, lambda e: e.dma_start(out=out, in_=in_), reads, writes, dma=semkey, group=group)

        def xkeys(cs, t0, ln):
            return [("x", c, b) for c in cs for b in range(t0 // 128, (t0 + ln + 127) // 128)]

        ALLC = list(range(NCH))

        ring = {"plan": [], "issued": 0, "cur": 0}

        def slot_keys(s):
            return [("ws", s, i) for i in range(MAXPARTS)]

        def issue_upto(k):
            while ring["issued"] < min(k, len(ring["plan"])):
                i = ring["issued"]
                s = i % NSLOT
                tag, parts = ring["plan"][i]
                for pi, (ovf, iv) in enumerate(parts):
                    wk = [("ws", s, pi)]
                    if pi == len(parts) - 1:
                        wk = [("ws", s, j) for j in range(pi, MAXPARTS)]
                    dma("pool", ovf(wr[s]), iv, f"ws{s}", (), wk, group=i)
                ring["issued"] += 1

        def acquire(tag):
            k = ring["cur"]
            assert ring["plan"][k][0] == tag, (ring["plan"][k][0], tag)
            issue_upto(k + NSLOT)
            ring["cur"] += 1
            s = k % NSLOT
            return wr[s], slot_keys(s)

        def spec_pool_w(l):
            return (("poolw", l), [(lambda w: w[:, 0:2048].rearrange("p (g k d) -> p g k d", g=4, k=2),
                                    dr["pool_w"][l].rearrange("g (k p) d -> p g k d", p=128))])

        def spec_up(l, pp):
            upv = dr["ffn_up"][l].rearrange("(k p) f -> p k f", p=128)
            parts = []
            for gv in range(2):
                c0 = gv * DFF + pp * 256
                parts.append(((lambda gv: lambda w: w[:, 0:4096].rearrange("p (k g c) -> p k g c", k=8, g=2)[:, :, gv, :])(gv),
                              upv[:, :, c0:c0 + 256]))
            return (("up", l, pp), parts)

        def spec_down(l, half, dq):
            f0 = 0 if half == 0 else HALF_PAIRS[0]
            nf = HALF_PAIRS[half]
            dv = dr["ffn_down"][l].rearrange("(f p) d -> p f d", p=128)
            return (("down", l, half, dq), [((lambda nf: lambda w: w[:, 0:nf * 256].rearrange("p (f d) -> p f d", f=nf))(nf),
                                             dv[:, f0:f0 + nf, dq * 256:(dq + 1) * 256])])

        def spec_wq(j, hf):
            v = dr["w_q"][j].rearrange("(k p) f -> p k f", p=128)
            return (("wq", j, hf), [(lambda w: w[:, 0:4096].rearrange("p (k c) -> p k c", k=8), v[:, :, hf * 512:(hf + 1) * 512])])

        def spec_wo(j, hf):
            v = dr["w_o"][j].rearrange("(k p) f -> p k f", p=128)
            return (("wo", j, hf), [(lambda w: w[:, 0:4096].rearrange("p (k c) -> p k c", k=8), v[:, :, hf * 512:(hf + 1) * 512])])

        def spec_wkv():
            v = dr["w_kv"].rearrange("(k p) f -> p k f", p=128)
            vk = dr["wk_pad"].rearrange("(k p) f -> p k f", p=128)
            return [(("wkvK",), [(lambda w: w[:, 0:4096].rearrange("p (k c) -> p k c", k=8), vk)]),
                    (("wkvV",), [(lambda w: w[:, 0:1024].rearrange("p (k c) -> p k c", k=8), v[:, :, 128:256])])]

        def stop_here(l, stage):
            return stop_after is not None and tuple(stop_after) == (l, stage)

        done = False
        for l in range(DEPTH):
            attn = l >= N_A
            for (t0, subs) in macro_tiles("attn" if attn else "pool"):
                if attn:
                    for _ in subs:
                        ring["plan"] += [spec_wq(l - N_A, 0), spec_wq(l - N_A, 1), spec_wo(l - N_A, 0), spec_wo(l - N_A, 1)]
                else:
                    ring["plan"].append(spec_pool_w(l))
                if not (stop_after is not None and tuple(stop_after) == (l, "M")):
                    pj = 0
                    for half in range(2):
                        for _ in range(HALF_PAIRS[half] // 2):
                            ring["plan"].append(spec_up(l, pj))
                            pj += 1
                        for dq in range(4):
                            ring["plan"].append(spec_down(l, half, dq))
                    if l == N_A - 1:
                        ring["plan"] += spec_wkv()
            if stop_after is not None and stop_after[0] == l:
                break

        xv = dr["xT"].rearrange("(c p) t -> p c t", p=128)
        for m in range(T // MACRO):
            for c in range(NCH):
                dma("sp", x_sb[:, c, m * MACRO:(m + 1) * MACRO], xv[:, c, m * MACRO:(m + 1) * MACRO], f"xl{m}",
                    (), xkeys([c], m * MACRO, MACRO), group=m)
            if m == 0:
                dma("sp", vecs[:], dr["vecs"], "cvec", (), ["vecs"])
                dma("sp", reps[:], dr["reps"], "crep", (), ["reps"])
        dma("pool", masks[:], dr["masks"], "cmask", (), ["masks"])
        dma("pool", ident[:], dr["ident"], "cident", (), ["ident"])
        issue_upto(NSLOT)
        mset("dve", ones[:], 1.0 / 1024.0, ["ones"])
        mset("pool", hn[:], 0.0, [("hn", c, s_) for c in ALLC for s_ in range(2)] + [("hnh", c) for c in ALLC])
        mset("pool", hn1[:], 0.0, [("h1", c, s_) for c in ALLC for s_ in range(2)] + [("h1h", c) for c in ALLC])
        mset("dve", epsc[:], EPS, ["epsc"])
        mset("dve", Vaug[:, :, :, 64:65], 1.0, ["vones"])
        act(esink[:], reps[:, rl["sink"]:rl["sink"] + 32], AF.Exp, ["reps"], ["esink"])

        def rsqrt_pn(sl, ri=0):
            rstd = rstd2[:, ri, :]
            rk = ("rstd", ri)
            act(rtmp[:, 0:sl], pn[:, 0:sl], AF.Ln, ["pn", "epsc"], ["rtmp"], bias=epsc[:, 0:1])
            act(rstd[:, 0:sl], rtmp[:, 0:sl], AF.Exp, ["rtmp"], [rk], scale=-0.5)

        def norm_stt(c, s0, sl, col0, si, dst, dkey, gname, gidx0):
            return lambda: stt("dve", dst[:, c, col0:col0 + sl], x_sb[:, c, s0:s0 + sl], vcol(gname, gidx0 + c), rstd2[:, si, 0:sl],
                               ALU.mult, ALU.mult, xkeys([c], s0, sl) + [("rstd", si), "vecs"], [(dkey, c, si)])

        def norm(s0, sl, t0, gname, gidx0, defer=False, dst=None, dkey="hn", pad=0, stt_later=False):
            dst = hn if dst is None else dst
            si = 0 if s0 < t0 + SUB else 1
            col0 = HOFF + (s0 - t0)
            todo = []
            D_ = todo.append
            D_(lambda: act(sq[:, 0:4, 0:sl], x_sb[:, 0:4, s0:s0 + sl], AF.Square, xkeys(range(0, 4), s0, sl), [("sq", 0)]))
            D_(lambda: act(sq[:, 4:8, 0:sl], x_sb[:, 4:8, s0:s0 + sl], AF.Square, xkeys(range(4, 8), s0, sl), [("sq", 1)]))
            for _ in range(pad):
                D_(lambda: None)

            def mms():
                for c in range(NCH):
                    mm(pn[:, 0:sl], ones[:], sq[:, c, 0:sl], c == 0, c == NCH - 1, [("sq", c // 4), "ones"], ["pn"])
            D_(mms)
            D_(lambda: rsqrt_pn(sl, si))
            if stt_later:
                return todo
            for c in range(NCH):
                D_(norm_stt(c, s0, sl, col0, si, dst, dkey, gname, gidx0))
            if defer:
                return todo
            for f in todo:
                f()
            return si

        def zero_halo(t0, subs):
            if t0 != 0:
                return
            lo = subs[0][0]
            ts("dve", x_sb[:, :, lo:HALO], x_sb[:, :, lo:HALO], vcol("flag", 0), None, ALU.mult, None,
               xkeys(ALLC, lo, HALO - lo) + ["vecs"], xkeys(ALLC, lo, HALO - lo))

        def pool_pre(l, t0, subs):
            L = MACRO
            W = HOFF + L
            todo = []
            D_ = todo.append
            if t0 == 0:
                D_(lambda: mset("pool", hn1[:, :, 0:HOFF], 0.0, [("h1h", c) for c in ALLC]))
            else:
                D_(lambda: cp("pool", hn1[:, :, 0:HOFF], carry1[:], ["carry1"], [("h1h", c) for c in ALLC]))
            for (s0, sl) in subs:
                todo.extend(norm(s0, sl, t0, "n1g", l * 8, defer=True, dst=hn1, dkey="h1", pad=PAD, stt_later=True))
            for c in range(NCH):
                nl = c // 2 + 1
                win = 2 ** nl
                hk = [("h1", c, 0), ("h1", c, 1), ("h1h", c)]
                for (s0, sl) in subs:
                    si_ = 0 if s0 < t0 + SUB else 1
                    D_(norm_stt(c, s0, sl, HOFF + (s0 - t0), si_, hn1, "h1", "n1g", l * 8))
                src = hn1[:, c, :]
                bufs = [pA, pB]
                cur = None
                for lev in range(nl):
                    sh = 2 ** lev
                    lo = 2 * sh - 1
                    dst = bufs[lev % 2]
                    dk = "pA" if lev % 2 == 0 else "pB"
                    if lev == 0:
                        D_((lambda dst, lo, sh, hk, dk, src: lambda: tt("dve", dst[:, lo:W], src[:, lo:W], src[:, lo - sh:W - sh], ALU.add, hk, [dk]))(dst, lo, sh, hk, dk, src))
                    else:
                        ck = "pA" if (lev - 1) % 2 == 0 else "pB"
                        D_((lambda dst, lo, sh, ck, dk, cur: lambda: tt("dve", dst[:, lo:W], cur[:, lo:W], cur[:, lo - sh:W - sh], ALU.add, [ck], [dk]))(dst, lo, sh, ck, dk, cur))
                    cur = dst
                ck = "pA" if (nl - 1) % 2 == 0 else "pB"
                D_((lambda c, cur, win, src, ck, hk: lambda: stt("dve", pooled[:, c, 0:L], cur[:, HOFF:W], 1.0 / win, src[:, HOFF:W], ALU.mult, ALU.subtract,
                                                                 [ck] + hk, [("pl", c, 0), ("pl", c, 1)]))(c, cur, win, src, ck, hk))
                if t0 == 0:
                    io = rl["invcnt"] + c * 16
                    D_((lambda c, cur, io, ck: lambda: tt("dve", ptmp[:], cur[:, HOFF + HALO:HOFF + HALO + 16], reps[:, io:io + 16], ALU.mult,
                                                          [ck, "reps"], ["ptmp"]))(c, cur, io, ck))
                    D_((lambda c, src, hk: lambda: tt("dve", pooled[:, c, HALO:HALO + 16], ptmp[:], src[:, HOFF + HALO:HOFF + HALO + 16], ALU.subtract,
                                                      ["ptmp"] + hk, [("pl", c, 0)]))(c, src, hk))
            D_(lambda: cp("pool", carry1[:], hn1[:, :, HOFF + L - 16:HOFF + L], [("h1", c, 1) for c in ALLC], ["carry1"]))
            return todo

        def pool_post(l, t0, subs):
            w, wk = acquire(("poolw", l))
            pw = w[:, 0:2048].rearrange("p (g k d) -> p g k d", g=4, k=2)
            for (s0, sl) in subs:
                si = 0 if s0 < t0 + SUB else 1
                c0 = s0 - t0
                for dc in range(NCH):
                    g = dc // 2
                    ps = pb[dc % 2]
                    for kc in range(2):
                        mm(ps[:, 0:sl], pw[:, g, kc, (dc % 2) * 128:(dc % 2) * 128 + 128], pooled[:, 2 * g + kc, c0:c0 + sl],
                           kc == 0, kc == 1, wk + [("pl", 2 * g + kc, si)], [("pb", dc % 2)])
                    stt("dve", x_sb[:, dc, s0:s0 + sl], ps[:, 0:sl], vcol("psc", l * 8 + dc), x_sb[:, dc, s0:s0 + sl],
                        ALU.mult, ALU.add, [("pb", dc % 2), "vecs"] + xkeys([dc], s0, sl), xkeys([dc], s0, sl))
            zero_halo(t0, subs)

        def ffn_stage(l, t0, subs, first_tile, inject=None):
            L = MACRO
            if first_tile:
                mset("pool", hn[:, :, HOFF - 2:HOFF], 0.0, [("hnh", c) for c in ALLC])
            else:
                cp("pool", hn[:, :, HOFF - 2:HOFF], carry2[:], ["carry2"], [("hnh", c) for c in ALLC])
            for (s0, sl) in subs:
                norm(s0, sl, t0, "n2g", l * 8)
            cp("pool", carry2[:], hn[:, :, HOFF + L - 2:HOFF + L], [("hn", c, 1) for c in ALLC], ["carry2"])
            hsub = [[[("hn", c, 0), ("hnh", c)] for c in ALLC],
                    [[("hn", c, 1), ("hn", c, 0)] for c in ALLC]]
            cwb = l * 3 * 44
            pj_base = 0
            fbuf = 0
            for half in range(2):
                npair = HALF_PAIRS[half]
                for pp in range(npair // 2):
                    w, wk = acquire(("up", l, pj_base // 2 + pp))
                    upw = w[:, 0:4096].rearrange("p (k g c) -> p k g c", k=8, g=2)
                    for (s0, sl) in subs:
                        for pj in range(2):
                            j = pj_base + pp * 2 + pj
                            jj = pp * 2 + pj
                            si = 0 if s0 < t0 + SUB else 1
                            c0 = HOFF + (s0 - t0)
                            pg = pa[2 * fbuf]
                            pv = pa[2 * fbuf + 1]
                            Ag, Av, Sg = ftmp[3 * fbuf], ftmp[3 * fbuf + 1], ftmp[3 * fbuf + 2]
                            kg, kv_, kA, kV, kS = ("pa", 2 * fbuf), ("pa", 2 * fbuf + 1), ("ft", 3 * fbuf), ("ft", 3 * fbuf + 1), ("ft", 3 * fbuf + 2)
                            for gv, (ps, pk) in enumerate([(pg, kg), (pv, kv_)]):
                                for k in range(NCH):
                                    mm(ps[:, 0:sl + 2], upw[:, k, gv, pj * 128:(pj + 1) * 128], hn[:, k, c0 - 2:c0 + sl],
                                       k == 0, k == NCH - 1, wk + hsub[si][k], [pk])
                            for gv, (ps, pk, A, ak) in enumerate([(pg, kg, Ag, kA), (pv, kv_, Av, kV)]):
                                fc = gv * NFC + j
                                w0 = vcol("cw", cwb + 0 * 44 + fc)
                                w1 = vcol("cw", cwb + 1 * 44 + fc)
                                w2 = vcol("cw", cwb + 2 * 44 + fc)
                                bb = vcol("cb", l * 44 + fc)
                                act(A[:, 0:sl], ps[:, 2:sl + 2], AF.Identity, [pk, "vecs"], [ak], bias=bb, scale=w2)
                                stt("dve", A[:, 0:sl], ps[:, 1:sl + 1], w1, A[:, 0:sl], ALU.mult, ALU.add, [pk, ak, "vecs"], [ak])
                                stt("dve", A[:, 0:sl], ps[:, 0:sl], w0, A[:, 0:sl], ALU.mult, ALU.add, [pk, ak, "vecs"], [ak])
                            act(Sg[:, 0:sl], Ag[:, 0:sl], AF.Silu, [kA], [kS])
                            tt("pool", G[:, jj, s0 - t0:s0 - t0 + sl], Sg[:, 0:sl], Av[:, 0:sl], ALU.mult, [kS, kV], [("G", jj, si)])
                            fbuf ^= 1
                bbanks = [(pb[0], ("pb", 0)), (pb[1], ("pb", 1)), (pa[2], ("pa", 2)), (pa[3], ("pa", 3))]
                bctr = 0
                if half == 0:
                    inj = list(inject) if inject is not None else []
                    ngroups = 16 * len(subs) - 4
                    per = (len(inj) + ngroups - 1) // ngroups if inj else 0
                for dq in range(4):
                    w, wk = acquire(("down", l, half, dq))
                    dw = w[:, 0:npair * 256].rearrange("p (f d) -> p f d", f=npair)
                    for dl in range(2):
                        dc = dq * 2 + dl
                        for (s0, sl) in subs:
                            si = 0 if s0 < t0 + SUB else 1
                            ps, pk = bbanks[bctr % 4]
                            bctr += 1
                            for f in range(npair):
                                mm(ps[:, 0:sl], dw[:, f, dl * 128:(dl + 1) * 128], G[:, f, s0 - t0:s0 - t0 + sl],
                                   f == 0, f == npair - 1, wk + [("G", f, si)], [pk])
                            tt("dve", x_sb[:, dc, s0:s0 + sl], ps[:, 0:sl], x_sb[:, dc, s0:s0 + sl], ALU.add,
                               [pk] + xkeys([dc], s0, sl), xkeys([dc], s0, sl))
                            for _ in range(per):
                                if inj:
                                    inj.pop(0)()
                if half == 1:
                    while inj:
                        inj.pop(0)()
                pj_base += npair
            zero_halo(t0, subs)

        def kv_stage(t0, subs):
            for (s0, sl) in subs:
                norm(s0, sl, t0, "kvg", 0)
            w, wk = acquire(("wkvK",))
            kw = w[:, 0:4096].rearrange("p (k g h c) -> p k g h c", k=8, g=2, h=2)
            for (s0, sl) in subs:
                si = 0 if s0 < t0 + SUB else 1
                c0 = HOFF + (s0 - t0)
                hk = [("hn", c, si) for c in ALLC]
                for g in range(2):
                    for hh in range(2):
                        for k in range(NCH):
                            mm(pb[hh][:, 0:sl], kw[:, k, g, hh, :], hn[:, k, c0:c0 + sl], k == 0, k == NCH - 1, wk + hk, [("pb", hh)])
                        act(KT[:, hh, g, s0:s0 + sl], pb[hh][:, 0:sl], AF.Identity, [("pb", hh), "vecs"],
                            [("KT", s0 // 128 + i) for i in range(sl // 128)], bias=vcol("bk", g * 2 + hh))
            w2, wk2 = acquire(("wkvV",))
            vw = w2[:, 0:1024].rearrange("p (k c) -> p k c", k=8)
            for (s0, sl) in subs:
                si = 0 if s0 < t0 + SUB else 1
                c0 = HOFF + (s0 - t0)
                hk = [("hn", c, si) for c in ALLC]
                for bi in range(sl // 128):
                    blk = s0 // 128 + bi
                    for k in range(NCH):
                        mm(pn[:, 0:128], hn[:, k, c0 + bi * 128:c0 + (bi + 1) * 128], vw[:, k, :], k == 0, k == NCH - 1, wk2 + hk, ["pn"])
                    tt("dve", Vaug[:, blk, :, 0:64], pn[:, 0:128].rearrange("p (g d) -> p g d", g=2),
                       reps[:, rl["bv"]:rl["bv"] + 128].rearrange("p (g d) -> p g d", g=2), ALU.add, ["pn", "reps", "vones"], [("V", blk)])

        OAK = [("pa", 2), ("pa", 3), "pn"]

        def oaug_ap(h):
            bank = [pa[2], pa[3], pn][h // 7]
            o = (h % 7) * 65
            return bank[:, o:o + 65], OAK[h // 7]

        def attn_pre(l, t0, subs):
            todo = []
            for (s0, sl) in subs:
                todo.extend(norm(s0, sl, t0, "n1g", l * 8, defer=True, dst=hn1, dkey="h1", pad=PAD))
            return todo

        def attn_stage(l, t0, subs):
            j = l - N_A
            QT = lambda c, a, b: G[:, c, a:b]
            OT = lambda c, a, b: G[:, c, SUB + a:SUB + b]
            for (s0, sl) in subs:
                si = 0 if s0 < t0 + SUB else 1
                c0 = HOFF + (s0 - t0)
                hk = [("h1", c, si) for c in ALLC]
                for hf in range(2):
                    w, wk = acquire(("wq", j, hf))
                    wv_ = w[:, 0:4096].rearrange("p (k c) -> p k c", k=8)
                    for cl in range(4):
                        c = hf * 4 + cl
                        ps, pk = pb[c % 2], ("pb", c % 2)
                        for k in range(NCH):
                            mm(ps[:, 0:sl], wv_[:, k, cl * 128:(cl + 1) * 128], hn1[:, k, c0:c0 + sl], k == 0, k == NCH - 1, wk + hk, [pk])
                        act(QT(c, 0, sl), ps[:, 0:sl], AF.Identity, [pk, "vecs"], [("G", c, 0)], bias=vcol("bq", j * 8 + c))
                nqb = sl // 128
                stbanks = [(pa[0], ("pa", 0)), (pa[1], ("pa", 1)), (pb[0], ("pb", 0)), (pb[1], ("pb", 1))]

                def qk(qb, hp):
                    n = s0 // 128 + qb
                    mview = masks[:, 2:4, :] if n == 2 else masks[:, 0:2, :]
                    g = hp // 4
                    stp, sk = stbanks[hp % 4]
                    for hh in range(2):
                        for kb in range(2):
                            kblk = n - 1 + kb
                            mm(stp[:, (hh * 2 + kb) * 128:(hh * 2 + kb + 1) * 128],
                               KT[:, hh, g, kblk * 128:(kblk + 1) * 128],
                               QT(hp, qb * 128, (qb + 1) * 128), True, True,
                               [("KT", kblk), ("G", hp, 0)], [sk])
                    pbuf = hp % NPT
                    Pk = ("Pt", pbuf)
                    act(Pt[pbuf], stp[:], AF.Exp, [sk], [Pk], scale=0.125)
                    pv4 = Pt[pbuf].rearrange("p (h k q) -> p h k q", h=2, k=2)
                    tt("dve", pv4, pv4, mview.unsqueeze(1).to_broadcast([128, 2, 2, 128]), ALU.mult, [Pk, "masks"], [Pk])

                def pvm(qb, hp):
                    n = s0 // 128 + qb
                    g = hp // 4
                    pbuf = hp % NPT
                    Pk = ("Pt", pbuf)
                    for hh in range(2):
                        h = hp * 2 + hh
                        oa, oak = oaug_ap(h)
                        for kb in range(2):
                            kblk = n - 1 + kb
                            mm(oa, Pt[pbuf][:, (hh * 2 + kb) * 128:(hh * 2 + kb + 1) * 128], Vaug[:, kblk, g, :],
                               kb == 0, kb == 1, [Pk, ("V", kblk), "vones"], [oak])

                def evac(qb, bi):
                    obuf = qb % 2
                    O = Osb[obuf]
                    h0, nh = [(0, 7), (7, 7), (14, 2)][bi]
                    bank = [pa[2], pa[3], pn][bi]
                    bv = bank[:, 0:nh * 65].rearrange("p (h d) -> p h d", h=nh)
                    dk, rk = ("den", bi), ("rden", bi)
                    tt("dve", den[:, h0:h0 + nh], bv[:, :, 64], esink[:, j * 16 + h0:j * 16 + h0 + nh], ALU.add,
                       [OAK[bi], "esink"], [dk])
                    P.add("dve", lambda e: e.reciprocal(rden[:, h0:h0 + nh], den[:, h0:h0 + nh]), [dk], [rk])
                    tt("dve", O[:, h0:h0 + nh, :], bv[:, :, 0:64], rden[:, h0:h0 + nh].unsqueeze(2).to_broadcast([128, nh, 64]),
                       ALU.mult, [OAK[bi], rk], [("Osb", obuf, bi)])

                def epilogue_pe(qb):
                    obuf = qb % 2
                    Of = Osb[obuf].rearrange("p h d -> p (h d)")
                    oks = [("Osb", obuf, bi) for bi in range(3)]
                    for c in range(NCH):
                        tr(pt[:, c * 128:(c + 1) * 128], Of[:, c * 128:(c + 1) * 128], oks, ["pt"])
                    act(G[:, 0:8, SUB + qb * 128:SUB + (qb + 1) * 128], pt[:].rearrange("p (c q) -> p c q", c=8), AF.Copy,
                        ["pt"], [("G", c, 1) for c in ALLC])

                units = [(qb, hp) for qb in range(nqb) for hp in range(8)]
                LA = 3
                for u in range(LA):
                    qk(*units[u])
                for u, (qb, hp) in enumerate(units):
                    if u + LA < len(units):
                        qk(*units[u + LA])
                    pvm(qb, hp)
                    if hp == 3:
                        evac(qb, 0)
                    if hp == 6:
                        evac(qb, 1)
                    if hp == 7:
                        evac(qb, 2)
                    if hp == 4 and qb > 0:
                        epilogue_pe(qb - 1)
                epilogue_pe(nqb - 1)
                o0 = max(s0, ATTN_FFN_START) - s0
                ol = sl - o0
                for hf in range(2):
                    w, wk = acquire(("wo", j, hf))
                    wv_ = w[:, 0:4096].rearrange("p (k c) -> p k c", k=8)
                    for dl in range(4):
                        dc = hf * 4 + dl
                        ps, pk = pb[dc % 2], ("pb", dc % 2)
                        for c in range(NCH):
                            mm(ps[:, 0:ol], wv_[:, c, dl * 128:(dl + 1) * 128], OT(c, o0, sl), c == 0, c == NCH - 1,
                               wk + [("G", c, 1)], [pk])
                        stt("dve", x_sb[:, dc, s0 + o0:s0 + sl], ps[:, 0:ol], vcol("bo", j * 8 + dc), x_sb[:, dc, s0 + o0:s0 + sl],
                            ALU.add, ALU.add, [pk, "vecs"] + xkeys([dc], s0, sl), xkeys([dc], s0, sl))
            zero_halo(t0, subs)

        def dump_x():
            dv = dbg.rearrange("(c p) t -> p c t", p=128)
            for c in range(NCH):
                dma("sp", dv[:, c, :], x_sb[:, c, :], "out", xkeys([c], 0, T), [("y", c)])
            P.add("sp", lambda e: e.nop(), [("y", c) for c in ALLC], ())

        stopped = False
        pre_done = set()
        fin = {"nout": 0, "oi": 0}

        def final_norm(t0):
            yv = yT.rearrange("(c p) t -> p c t", p=128)
            obufs = [(pA[:, 0:SUB], "pA"), (pA[:, SUB:2 * SUB], "pA2"), (pB[:, 0:SUB], "pB"), (pB[:, SUB:2 * SUB], "pB2")]
            for (s0, sl) in [(t0, SUB), (t0 + SUB, SUB)]:
                if s0 + sl <= HALO:
                    continue
                if s0 < HALO:
                    s0, sl = HALO, s0 + sl - HALO
                act(sq[:, 0:4, 0:sl], x_sb[:, 0:4, s0:s0 + sl], AF.Square, xkeys(range(0, 4), s0, sl), [("sq", 0)])
                act(sq[:, 4:8, 0:sl], x_sb[:, 4:8, s0:s0 + sl], AF.Square, xkeys(range(4, 8), s0, sl), [("sq", 1)])
                for c in range(NCH):
                    mm(pn[:, 0:sl], ones[:], sq[:, c, 0:sl], c == 0, c == NCH - 1, [("sq", c // 4), "ones"], ["pn"])
                rsqrt_pn(sl, 0)
                for c in range(NCH):
                    ob, okk = obufs[fin["oi"] % 4]
                    semk = f"out{fin['oi'] % 4}"
                    fin["oi"] += 1
                    stt("dve", ob[:, 0:sl], x_sb[:, c, s0:s0 + sl], vcol("fing", c), rstd2[:, 0, 0:sl], ALU.mult, ALU.mult,
                        xkeys([c], s0, sl) + [("rstd", 0), "vecs"], [okk])
                    dma("sp", yv[:, c, s0 - HALO:s0 - HALO + sl], ob[:, 0:sl], semk, [okk], [("y", fin["nout"])])
                    fin["nout"] += 1

        def pre_of(l, mi):
            attn = l >= N_A
            t0, subs = macro_tiles("attn" if attn else "pool")[mi]
            return attn_pre(l, t0, subs) if attn else pool_pre(l, t0, subs)

        for l in range(DEPTH):
            attn = l >= N_A
            tiles = macro_tiles("attn" if attn else "pool")
            for mi, (t0, subs) in enumerate(tiles):
                fsubs = macro_tiles("attn_ffn")[mi][1] if attn else subs
                if (l, mi) not in pre_done:
                    for f in pre_of(l, mi):
                        f()
                if attn:
                    attn_stage(l, t0, subs)
                else:
                    pool_post(l, t0, subs)
                if stop_here(l, "M"):
                    continue
                inject = None
                nxt = (l, mi + 1) if mi + 1 < len(tiles) else (l + 1, 0)
                if nxt[0] < DEPTH and not (stop_after is not None and stop_after[0] == l and nxt[0] != l):
                    inject = pre_of(*nxt)
                    pre_done.add(nxt)
                ffn_stage(l, t0, fsubs, mi == 0, inject)
                if l == N_A - 1:
                    kv_stage(t0, macro_tiles("std")[mi][1])
                if l == DEPTH - 1 and stop_after is None:
                    final_norm(t0)
            if stop_after is not None and stop_after[0] == l:
                dump_x()
                stopped = True
                break

        if not stopped:
            P.add("sp", lambda e: e.nop(), [("y", i) for i in range(fin["nout"])], ())

        assert ring["cur"] == len(ring["plan"]), (ring["cur"], len(ring["plan"]))
        sems = {e: st.enter_context(nc.semaphore("s_" + e)) for e in ENG_NAMES}
        dma_sems = {k: st.enter_context(nc.semaphore("d_" + k)) for k in P.dma_keys()}
        block = st.enter_context(nc.Block())
        P.emit(sems, dma_sems, block)
    return nc


_NC_CACHE = {}


def kernel(**inputs):
    in_maps = host_prep(inputs)
    if "full" not in _NC_CACHE:
        _NC_CACHE["full"] = build_program(None)
    nc = _NC_CACHE["full"]
    res = run_bass_kernel_spmd(nc, in_maps, core_ids=list(range(NCORES)))
    out = np.empty((2, SEQ, D), np.float32)
    for core in range(NCORES):
        b = core // 4
        t0 = (core % 4) * TOWN
        out[b, t0:t0 + TOWN, :] = np.asarray(res.results[core]["yT"]).T
    return out
```
